# Optimizing a Trainium2 kernel written in Bass

```python
import math
import jax, jax.numpy as jnp
from jax import lax
import numpy as np

D_MODEL = 1024
BATCH = 4
SEQ = 8192
DEPTH = 1
DEC_BATCH = 2
DEC_SEQ = 8192
PAST_LEN = 128

N_HEADS = 16
N_KV_HEADS = 4
HEAD_DIM = 64
GQA_GROUP = N_HEADS // N_KV_HEADS
ATTN_DIM = N_HEADS * HEAD_DIM
KV_DIM = N_KV_HEADS * HEAD_DIM
AXIS_ROT_DIM = HEAD_DIM // 2
ROPE_THETA = 10000.0
GRID_W = 64
Q_BLOCK = 128

SSM_EXPAND = 2
D_INNER = SSM_EXPAND * D_MODEL
SSM_HEADDIM = 64
N_SSM_HEADS = D_INNER // SSM_HEADDIM
N_SSM_GROUPS = 4
HEADS_PER_GROUP = N_SSM_HEADS // N_SSM_GROUPS
D_STATE = 128
SSM_CONV = 3
CONV_DIM = D_INNER + 2 * N_SSM_GROUPS * D_STATE
CHUNK = 128

D_FF = 2816
FFN_CONV = 3

PLE_DIM = 256

NORM_EPS = 1e-6

IN_SIZES = (ATTN_DIM, KV_DIM, KV_DIM, D_INNER, CONV_DIM, 2 * N_SSM_HEADS, 2 * D_MODEL)
IN_DIM = ATTN_DIM + 2 * KV_DIM + D_INNER + CONV_DIM + 2 * N_SSM_HEADS + 2 * D_MODEL

kernel_name = 'hybrid_ssd_axial_gqa_encoder'


def _rms_norm(x, w):
    xf = x.astype(jnp.float32)
    y = xf * lax.rsqrt(jnp.mean(xf * xf, axis=-1, keepdims=True) + NORM_EPS) * w.astype(jnp.float32)
    return y.astype(x.dtype)


def _depthwise_conv(x, w, b):
    k = w.shape[0]
    y = lax.conv_general_dilated(
        x, w[:, None, :], window_strides=(1,), padding=[(k // 2, k // 2)],
        dimension_numbers=('NWC', 'WIO', 'NWC'), feature_group_count=x.shape[-1])
    return y + b


def _split_columns(proj):
    parts = []
    start = 0
    for size in IN_SIZES:
        parts.append(proj[..., start:start + size])
        start += size
    return parts


def _axial_rope(n_tokens):
    rows = n_tokens // GRID_W
    row_idx = jnp.repeat(jnp.arange(rows, dtype=jnp.float32), GRID_W)
    col_idx = jnp.tile(jnp.arange(GRID_W, dtype=jnp.float32), rows)
    inv_freq = ROPE_THETA ** (-jnp.arange(0, AXIS_ROT_DIM, 2, dtype=jnp.float32) / AXIS_ROT_DIM)
    ang = jnp.concatenate([row_idx[:, None] * inv_freq, col_idx[:, None] * inv_freq], axis=-1)
    return jnp.cos(ang), jnp.sin(ang)


def _apply_rope(x, cos, sin):
    shape = (1, x.shape[1]) + (1,) * (x.ndim - 3) + (HEAD_DIM // 2,)
    c = cos.reshape(shape)
    s = sin.reshape(shape)
    xf = x.astype(jnp.float32).reshape(x.shape[:-1] + (HEAD_DIM // 2, 2))
    x0 = xf[..., 0]
    x1 = xf[..., 1]
    out = jnp.stack([x0 * c - x1 * s, x0 * s + x1 * c], axis=-1).reshape(x.shape)
    return out.astype(x.dtype)


def _block_attention(q, k, v):
    bsz, s = q.shape[0], q.shape[1]
    nblk = s // Q_BLOCK
    qb = q.reshape(bsz, nblk, Q_BLOCK, N_KV_HEADS, GQA_GROUP, HEAD_DIM).swapaxes(0, 1)
    scale = HEAD_DIM ** -0.5

    def one_block(qblk):
        scores = jnp.einsum('bqkgd,bskd->bkgqs', qblk, k, preferred_element_type=jnp.float32) * scale
        probs = jax.nn.softmax(scores, axis=-1).astype(v.dtype)
        return jnp.einsum('bkgqs,bskd->bqkgd', probs, v)

    out = lax.map(one_block, qb)
    return out.swapaxes(0, 1).reshape(bsz, s, ATTN_DIM)


def _attn_mixer(q, k, v, q_norm_w, k_norm_w, cos, sin):
    bsz, s = q.shape[0], q.shape[1]
    q = q.reshape(bsz, s, N_KV_HEADS, GQA_GROUP, HEAD_DIM)
    k = k.reshape(bsz, s, N_KV_HEADS, HEAD_DIM)
    v = v.reshape(bsz, s, N_KV_HEADS, HEAD_DIM)
    q = _apply_rope(_rms_norm(q, q_norm_w), cos, sin)
    k = _apply_rope(_rms_norm(k, k_norm_w), cos, sin)
    return _block_attention(q, k, v)


def _ssd_scan(x, dt, a, b_mat, c_mat):
    bsz, s = x.shape[0], x.shape[1]
    nc = s // CHUNK
    xs = x.reshape(bsz, nc, CHUNK, N_SSM_GROUPS, HEADS_PER_GROUP, SSM_HEADDIM)
    dts = dt.reshape(bsz, nc, CHUNK, N_SSM_GROUPS, HEADS_PER_GROUP)
    bc = b_mat.reshape(bsz, nc, CHUNK, N_SSM_GROUPS, D_STATE)
    cc = c_mat.reshape(bsz, nc, CHUNK, N_SSM_GROUPS, D_STATE)
    xdt = xs * dts[..., None]
    a_cum = jnp.cumsum((dts * a).transpose(0, 3, 4, 1, 2), axis=-1)
    seg = a_cum[..., :, None] - a_cum[..., None, :]
    lower = jnp.tril(jnp.ones((CHUNK, CHUNK), dtype=bool))
    decay = jnp.where(lower, jnp.exp(jnp.where(lower, seg, 0.0)), 0.0)
    cb = jnp.einsum('bclgn,bcsgn->bgcls', cc, bc)
    y_diag = jnp.einsum('bgcls,bgrcls,bcsgrp->bclgrp', cb, decay, xdt)
    decay_states = jnp.exp(a_cum[..., -1:] - a_cum)
    states = jnp.einsum('bclgn,bgrcl,bclgrp->bcgrpn', bc, decay_states, xdt)
    chunk_decay = jnp.exp(a_cum[..., -1])

    def step(h, inp):
        st, dec = inp
        return h * dec[..., None, None] + st, h

    h0 = jnp.zeros((bsz, N_SSM_GROUPS, HEADS_PER_GROUP, SSM_HEADDIM, D_STATE), jnp.float32)
    _, h_in = lax.scan(step, h0, (jnp.moveaxis(states, 1, 0), jnp.moveaxis(chunk_decay, -1, 0)))
    y_off = jnp.einsum('bclgn,cbgrpn,bgrcl->bclgrp', cc, h_in, jnp.exp(a_cum))
    return (y_diag + y_off).reshape(bsz, s, N_SSM_GROUPS, HEADS_PER_GROUP, SSM_HEADDIM)


def _ssd_mixer(z, xbc, dt_raw, conv_w, conv_b, dt_bias, a_log, d_skip, norm_w):
    bsz, s = z.shape[0], z.shape[1]
    xbc = jax.nn.silu(_depthwise_conv(xbc, conv_w, conv_b))
    gn = N_SSM_GROUPS * D_STATE
    f32 = jnp.float32
    x5 = xbc[..., :D_INNER].astype(f32).reshape(bsz, s, N_SSM_GROUPS, HEADS_PER_GROUP, SSM_HEADDIM)
    b4 = xbc[..., D_INNER:D_INNER + gn].astype(f32).reshape(bsz, s, N_SSM_GROUPS, D_STATE)
    c4 = xbc[..., D_INNER + gn:].astype(f32).reshape(bsz, s, N_SSM_GROUPS, D_STATE)
    dt_all = jax.nn.softplus(dt_raw.astype(f32).reshape(bsz, s, 2, N_SSM_HEADS) + dt_bias.astype(f32))
    dt_all = dt_all.reshape(bsz, s, 2, N_SSM_GROUPS, HEADS_PER_GROUP)
    a = -jnp.exp(a_log.astype(f32)).reshape(2, N_SSM_GROUPS, HEADS_PER_GROUP)
    y_fwd = _ssd_scan(x5, dt_all[:, :, 0], a[0], b4, c4)
    flip = lambda t: jnp.flip(t, axis=1)
    y_bwd = flip(_ssd_scan(flip(x5), flip(dt_all[:, :, 1]), a[1], flip(b4), flip(c4)))
    y = y_fwd + y_bwd + x5 * d_skip.astype(f32).reshape(N_SSM_GROUPS, HEADS_PER_GROUP)[..., None]
    gsz = HEADS_PER_GROUP * SSM_HEADDIM
    y = y.reshape(bsz, s, N_SSM_GROUPS, gsz) * jax.nn.silu(z.astype(f32)).reshape(bsz, s, N_SSM_GROUPS, gsz)
    y = y * lax.rsqrt(jnp.mean(y * y, axis=-1, keepdims=True) + NORM_EPS) * norm_w.astype(f32).reshape(N_SSM_GROUPS, gsz)
    return y.reshape(bsz, s, D_INNER).astype(z.dtype)


def _conv_ffn(h, w_up, conv_w, conv_b, w_down):
    u = _depthwise_conv(h @ w_up, conv_w, conv_b)
    gate = u[..., :D_FF]
    val = u[..., D_FF:]
    return (jax.nn.gelu(gate, approximate=True) * val) @ w_down


def _trunk(x, p, norm1_w, w_in, ssm_conv_w, ssm_conv_b, dt_bias, a_log, d_skip, ssd_norm_w,
           q_norm_w, k_norm_w, gate_b, w_attn_branch, w_ssd_branch, w_out, norm2_w, w_up,
           ffn_conv_w, ffn_conv_b, w_down, w_ple, w_ple_gate, b_ple_gate, final_norm_w):
    cos, sin = _axial_rope(x.shape[1])
    for i in range(DEPTH):
        h = _rms_norm(x, norm1_w[i])
        q, k, v, z, xbc, dt_raw, gates = _split_columns(h @ w_in[i])
        attn = _attn_mixer(q, k, v, q_norm_w[i], k_norm_w[i], cos, sin)
        ssd = _ssd_mixer(z, xbc, dt_raw, ssm_conv_w[i], ssm_conv_b[i], dt_bias[i], a_log[i], d_skip[i], ssd_norm_w[i])
        g = jax.nn.sigmoid(gates + gate_b[i])
        merged = g[..., :D_MODEL] * (attn @ w_attn_branch[i]) + g[..., D_MODEL:] * (ssd @ w_ssd_branch[i])
        x = x + merged @ w_out[i]
        x = x + _conv_ffn(_rms_norm(x, norm2_w[i]), w_up[i], ffn_conv_w[i], ffn_conv_b[i], w_down[i])
        x = x + jax.nn.sigmoid(x @ w_ple_gate[i] + b_ple_gate[i]) * (p[i] @ w_ple[i])
    return _rms_norm(x, final_norm_w)


def setup_inputs(seed: int = 0) -> dict:
    key = jax.random.key(seed)
    ks = jax.random.split(key, 32)
    f32 = jnp.float32

    def nrm(k, shape, scale):
        return jax.random.normal(k, shape, f32) * scale

    dt0 = jnp.exp(jax.random.uniform(ks[10], (DEPTH, 2, N_SSM_HEADS), f32, math.log(1e-3), math.log(1e-1)))
    return {
        'x_prompt': nrm(ks[0], (BATCH, SEQ, D_MODEL), 1.0),
        'x_sample': nrm(ks[1], (DEC_BATCH, DEC_SEQ, D_MODEL), 1.0),
        'p_prompt': nrm(ks[2], (DEPTH, BATCH, SEQ, PLE_DIM), 1.0),
        'p_sample': nrm(ks[3], (DEPTH, DEC_BATCH, DEC_SEQ, PLE_DIM), 1.0),
        'norm1_w': 1.0 + nrm(ks[4], (DEPTH, D_MODEL), 0.05),
        'w_in': nrm(ks[5], (DEPTH, D_MODEL, IN_DIM), D_MODEL ** -0.5),
        'ssm_conv_w': nrm(ks[6], (DEPTH, SSM_CONV, CONV_DIM), SSM_CONV ** -0.5),
        'ssm_conv_b': nrm(ks[7], (DEPTH, CONV_DIM), 0.01),
        'dt_bias': dt0 + jnp.log(-jnp.expm1(-dt0)),
        'a_log': jnp.log(jax.random.uniform(ks[11], (DEPTH, 2, N_SSM_HEADS), f32, 1.0, 16.0)),
        'd_skip': 1.0 + nrm(ks[12], (DEPTH, N_SSM_HEADS), 0.1),
        'ssd_norm_w': 1.0 + nrm(ks[13], (DEPTH, D_INNER), 0.05),
        'q_norm_w': 1.0 + nrm(ks[14], (DEPTH, HEAD_DIM), 0.05),
        'k_norm_w': 1.0 + nrm(ks[15], (DEPTH, HEAD_DIM), 0.05),
        'gate_b': nrm(ks[16], (DEPTH, 2 * D_MODEL), 0.01),
        'w_attn_branch': nrm(ks[17], (DEPTH, ATTN_DIM, D_MODEL), ATTN_DIM ** -0.5),
        'w_ssd_branch': nrm(ks[18], (DEPTH, D_INNER, D_MODEL), D_INNER ** -0.5),
        'w_out': nrm(ks[19], (DEPTH, D_MODEL, D_MODEL), D_MODEL ** -0.5),
        'norm2_w': 1.0 + nrm(ks[20], (DEPTH, D_MODEL), 0.05),
        'w_up': nrm(ks[21], (DEPTH, D_MODEL, 2 * D_FF), D_MODEL ** -0.5),
        'ffn_conv_w': nrm(ks[22], (DEPTH, FFN_CONV, 2 * D_FF), FFN_CONV ** -0.5),
        'ffn_conv_b': nrm(ks[23], (DEPTH, 2 * D_FF), 0.01),
        'w_down': nrm(ks[24], (DEPTH, D_FF, D_MODEL), D_FF ** -0.5),
        'w_ple': nrm(ks[25], (DEPTH, PLE_DIM, D_MODEL), PLE_DIM ** -0.5),
        'w_ple_gate': nrm(ks[26], (DEPTH, D_MODEL, D_MODEL), D_MODEL ** -0.5),
        'b_ple_gate': nrm(ks[27], (DEPTH, D_MODEL), 0.01),
        'final_norm_w': 1.0 + nrm(ks[28], (D_MODEL,), 0.05),
    }


def reference(x_prompt, x_sample, p_prompt, p_sample, norm1_w, w_in, ssm_conv_w, ssm_conv_b, dt_bias,
              a_log, d_skip, ssd_norm_w, q_norm_w, k_norm_w, gate_b, w_attn_branch, w_ssd_branch, w_out,
              norm2_w, w_up, ffn_conv_w, ffn_conv_b, w_down, w_ple, w_ple_gate, b_ple_gate, final_norm_w):
    y_prompt = _trunk(x_prompt, p_prompt, norm1_w, w_in, ssm_conv_w, ssm_conv_b, dt_bias, a_log, d_skip,
                      ssd_norm_w, q_norm_w, k_norm_w, gate_b, w_attn_branch, w_ssd_branch, w_out, norm2_w,
                      w_up, ffn_conv_w, ffn_conv_b, w_down, w_ple, w_ple_gate, b_ple_gate, final_norm_w)
    y_sample = _trunk(x_sample, p_sample, norm1_w, w_in, ssm_conv_w, ssm_conv_b, dt_bias, a_log, d_skip,
                      ssd_norm_w, q_norm_w, k_norm_w, gate_b, w_attn_branch, w_ssd_branch, w_out, norm2_w,
                      w_up, ffn_conv_w, ffn_conv_b, w_down, w_ple, w_ple_gate, b_ple_gate, final_norm_w)
    return (y_prompt, y_sample)
```

```python
import numpy as np
from contextlib import ExitStack
import concourse.bass as bass
import concourse.mybir as mybir
from concourse.bass_utils import run_bass_kernel_spmd
from concourse.alu_op_type import AluOpType as ALU

F32 = mybir.dt.float32
BF16 = mybir.dt.bfloat16
AF = mybir.ActivationFunctionType
ENGS = ("pe", "act", "dve", "pool", "sp")
EPS = 1e-6
D = 1024


class Buf:
    __slots__ = ("name", "w", "r")

    def __init__(self, name):
        self.name = name
        self.w = None
        self.r = []


class Op:
    __slots__ = ("eng", "fn", "deps", "sig", "semkey", "val", "is_dma")

    def __init__(self, eng, fn, is_dma, semkey):
        self.eng = eng
        self.fn = fn
        self.deps = []
        self.sig = False
        self.semkey = semkey
        self.val = None
        self.is_dma = is_dma


class T:
    __slots__ = ("ap", "buf")

    def __init__(self, ap, buf):
        self.ap = ap
        self.buf = buf

    def __getitem__(self, idx):
        return T(self.ap[idx], self.buf)

    def v(self, fn):
        return T(fn(self.ap), self.buf)


class Prog:
    def __init__(self, nc):
        self.nc = nc
        self.ops = {e: [] for e in ENGS}
        self.bufs = {}
        self.dmas = []

    def buf(self, name):
        b = self.bufs.get(name)
        if b is None:
            b = self.bufs[name] = Buf(name)
        return b

    def _bl(self, xs):
        out = []
        for x in xs:
            if x is None or isinstance(x, (int, float)):
                continue
            if isinstance(x, (list, tuple)):
                out.extend(self._bl(x))
            elif isinstance(x, T):
                out.append(self.buf(x.buf))
            elif isinstance(x, str):
                out.append(self.buf(x))
        return out

    def op(self, eng, fn, reads=(), writes=(), dma=None, extra=()):
        is_dma = dma is not None
        o = Op(eng, fn, is_dma, dma if is_dma else eng)
        R = self._bl(reads)
        W = self._bl(writes)
        deps = {}
        for b in R:
            if b.w is not None:
                deps[id(b.w)] = b.w
            if b.name.startswith("ps") and not is_dma:
                for r in b.r:
                    if r.eng != eng:
                        deps[id(r)] = r
        for b in W:
            if b.w is not None:
                deps[id(b.w)] = b.w
            for r in b.r:
                deps[id(r)] = r
        for d in extra:
            deps[id(d)] = d
        for d in deps.values():
            if d is o or d.fn is None:
                continue
            if (not d.is_dma) and (not is_dma) and d.eng == eng and fn is not None:
                if eng == "pe":
                    continue
                if not any(b.w is d for b in R):
                    continue
            o.deps.append(d)
            d.sig = True
        for b in R:
            b.r.append(o)
        for b in W:
            b.w = o
            b.r = []
        self.ops[eng].append(o)
        if is_dma:
            self.dmas.append(o)
        return o

    def barrier(self, dummies):
        dm = list(self.dmas)
        self.dmas = []
        self.op("act", lambda e: e.activation(out=dummies["act"], in_=dummies["act"], func=AF.Copy), writes=["bar_act"])
        self.op("dve", lambda e: e.memset(dummies["dve"], 0.0), writes=["bar_dve"])
        self.op("pool", lambda e: e.memset(dummies["pool"], 0.0), writes=["bar_pool"])
        for e in ENGS:
            self.op(e, None, reads=["bar_act", "bar_dve", "bar_pool"], extra=dm)

    def emit(self):
        nc = self.nc
        counts = {}
        keys = []
        for e in ENGS:
            for o in self.ops[e]:
                if o.fn is None:
                    continue
                if o.is_dma:
                    o.sig = True
                if o.sig:
                    k = o.semkey
                    if k not in counts:
                        counts[k] = 0
                        keys.append(k)
                    counts[k] += 16 if o.is_dma else 1
                    o.val = counts[k]
        final = list(self.dmas)
        with ExitStack() as st:
            sems = {k: st.enter_context(nc.semaphore("s_" + str(k))) for k in keys}
            block = st.enter_context(nc.Block())
            handles = {"pe": "tensor", "act": "scalar", "dve": "vector", "pool": "gpsimd", "sp": "sync"}

            def run(ename, eng):
                waited = {}

                def wait_for(dl):
                    need = {}
                    for d in dl:
                        if d.val > need.get(d.semkey, 0):
                            need[d.semkey] = d.val
                    for k, v in need.items():
                        if waited.get(k, 0) < v:
                            eng.wait_ge(sems[k], v)
                            waited[k] = v

                for o in self.ops[ename]:
                    wait_for(o.deps)
                    if o.fn is None:
                        continue
                    ins = o.fn(eng)
                    if o.sig:
                        ins.then_inc(sems[o.semkey], 16 if o.is_dma else 1)
                if ename == "sp":
                    wait_for(final)

            for ename in ENGS:
                def mk(ename=ename):
                    def _f(eng):
                        run(ename, eng)
                    return _f
                getattr(block, handles[ename])(mk())
        self.n_ops = {e: len(self.ops[e]) for e in ENGS}


class Arena:
    def __init__(self, t, nwords):
        self.t = t
        self.n = nwords
        self.off = 0

    def f32(self, n):
        assert self.off + n <= self.n, ("arena overflow", self.off, n, self.n)
        ap = self.t[:, self.off:self.off + n]
        self.off += n
        return ap

    def bf16(self, n):
        nw = (n + 1) // 2
        assert self.off + nw <= self.n, ("arena overflow", self.off, nw, self.n)
        ap = self.t[:, self.off:self.off + nw].bitcast(BF16)
        self.off += nw
        return ap


def r3(ap, a):
    return ap.rearrange("p (a b) -> p a b", a=a)


WSPEC = [
    ("w_q", 1024, 1024, 512), ("w_k", 1024, 256, None), ("w_v", 1024, 256, None), ("w_z", 1024, 2048, None), ("w_xbc", 1024, 3072, 512),
    ("w_dt", 1024, 64, None), ("w_g", 1024, 2048, 128), ("w_a", 1024, 1024, 128), ("w_s", 2048, 1024, 128), ("w_o", 1024, 1024, 512),
    ("w_up", 1024, 5632, 128), ("w_dn", 2816, 1024, 128), ("w_pl", 256, 1024, 128), ("w_pg", 1024, 1024, 128),
]
WMW = {n: mw for n, k, m, mw in WSPEC}


def wshape(k, m, mw):
    return [k, m] if mw is None else [(m // mw) * 128, (k // 128) * mw]


NCONST = 9
CV = {}
_cv_off = 0
for _n, _w in [("n1", 8), ("n2", 8), ("nf", 8), ("gb", 16), ("bpg", 8), ("qw", 1), ("kw", 1), ("cw0", 24), ("cw1", 24), ("cw2", 24),
               ("cb", 24), ("fw0", 44), ("fw1", 44), ("fw2", 44), ("fb", 44), ("dtb", 1), ("alog", 1)]:
    CV[_n] = (_cv_off, _w)
    _cv_off += _w
NCV = _cv_off


def build(S, debug=False):
    NT = S // 128
    TT = 512 if S >= 512 else S
    NTT = S // TT
    CPT = TT // 128
    nc = bass.Bass("TRN2", target_bir_lowering=False)
    P = Prog(nc)
    dt_in = lambda n, sh, d=F32: nc.dram_tensor(n, sh, d, kind="ExternalInput").ap()
    okind = "ExternalOutput" if debug else "Internal"
    dt_sc = lambda n, sh, d: nc.dram_tensor(n, sh, d, kind=okind).ap()
    x_d = dt_in("x", [S, D])
    p_d = dt_in("p", [S, 256])
    wsrc = {n: dt_in(n, wshape(k, m, mw)) for n, k, m, mw in WSPEC}
    consts_d = dt_in("consts", [128, NCONST, 128])
    cv_d = dt_in("cvec", [128, NCV])
    rows_d = dt_in("rows", [1, 1024 + 2048 + 32])
    cos_d = dt_in("cosT", [128, S])
    sin_d = dt_in("sinT", [128, S])
    y_d = nc.dram_tensor("y", [S, D], F32, kind="ExternalOutput").ap()
    wb = {n: dt_sc(n + "_b", wshape(k, m, mw), BF16) for n, k, m, mw in WSPEC}
    hT_d = dt_sc("hT_d", [128, 8, S + 2], BF16)
    xT_d = dt_sc("xT_d", [128, 8, S], F32)
    pT_d = dt_sc("pT_d", [128, 2, S], BF16)
    xtok_d = dt_sc("xtok_d", [S, 2048], BF16)
    btok_d = dt_sc("btok_d", [S, 512], BF16)
    bT_d = dt_sc("bT_d", [128, 4, S], BF16)
    cT_d = dt_sc("cT_d", [128, 4, S], BF16)
    dd_d = dt_sc("dd_d", [S, 128], F32)
    yf_d = dt_sc("yf_d", [S, 2048], F32)
    ssdT_d = dt_sc("ssdT_d", [128, 16, S], BF16)
    x1T_d = dt_sc("x1T_d", [128, 8, S], F32)
    h2T_d = dt_sc("h2T_d", [128, 8, S + 2], BF16)
    rs_d = nc.dram_tensor("rs_d", [4, 512], F32, kind="Internal").ap()

    with ExitStack() as st:
        NW_ALL = 51000
        sb_t = st.enter_context(nc.sbuf_tensor("sb", [128, NW_ALL], F32))
        ps_t = st.enter_context(nc.psum_tensor("psum", [128, 4096], F32))
        RES = Arena(sb_t, NW_ALL)
        PS = [T(ps_t[:, b * 512:(b + 1) * 512], "ps%d" % b) for b in range(8)]
        psbf = lambda b: T(ps_t[:, b * 512:(b + 1) * 512].bitcast(BF16), "ps%d" % b)
        ps2 = lambda b: T(ps_t[:, b * 512:(b + 2) * 512], "ps%d" % b)

        def bufs(*xs):
            return [x for x in xs if isinstance(x, (T, str))]

        def apof(x):
            return x.ap if isinstance(x, T) else x

        def mm(out, lhsT, rhs, start=True, stop=True, extra_w=()):
            P.op("pe", lambda e: e.matmul(out.ap, lhsT.ap, rhs.ap, start=start, stop=stop), reads=[lhsT, rhs], writes=[out, *extra_w])

        def tr(out, in_, ident, extra_w=()):
            P.op("pe", lambda e: e.transpose(out.ap, in_.ap, ident.ap), reads=[in_, ident], writes=[out, *extra_w])

        def act(out, in_, func, scale=None, bias=None, accum=None, extra_r=(), extra_w=()):
            kw = {}
            if scale is not None:
                kw["scale"] = apof(scale)
            if bias is not None:
                kw["bias"] = apof(bias)
            if accum is not None:
                kw["accum_out"] = accum.ap
            P.op("act", lambda e: e.activation(out=out.ap, in_=in_.ap, func=func, **kw),
                 reads=[in_, *bufs(scale, bias), *extra_r], writes=[out, *bufs(accum), *extra_w])

        def tt(eng, out, a, b, op, extra_r=(), extra_w=()):
            P.op(eng, lambda e: e.tensor_tensor(out=out.ap, in0=a.ap, in1=b.ap, op=op), reads=[a, b, *extra_r], writes=[out, *extra_w])

        def stt(eng, out, a, scalar, b, op0, op1, extra_r=()):
            P.op(eng, lambda e: e.scalar_tensor_tensor(out=out.ap, in0=a.ap, scalar=apof(scalar), in1=b.ap, op0=op0, op1=op1),
                 reads=[a, b, *bufs(scalar), *extra_r], writes=[out])

        def ts(eng, out, a, s1, s2, op0, op1=None, extra_r=()):
            if op1 is None:
                P.op(eng, lambda e: e.tensor_scalar(out=out.ap, in0=a.ap, scalar1=apof(s1), scalar2=None, op0=op0),
                     reads=[a, *bufs(s1), *extra_r], writes=[out])
            else:
                P.op(eng, lambda e: e.tensor_scalar(out=out.ap, in0=a.ap, scalar1=apof(s1), scalar2=apof(s2), op0=op0, op1=op1),
                     reads=[a, *bufs(s1, s2), *extra_r], writes=[out])

        def cp(eng, out, in_, extra_r=(), extra_w=()):
            if eng == "act":
                P.op("act", lambda e: e.copy(out=out.ap, in_=in_.ap), reads=[in_, *extra_r], writes=[out, *extra_w])
            else:
                P.op(eng, lambda e: e.tensor_copy(out=out.ap, in_=in_.ap), reads=[in_, *extra_r], writes=[out, *extra_w])

        def recip(out, in_):
            P.op("dve", lambda e: e.reciprocal(out=out.ap, in_=in_.ap), reads=[in_], writes=[out])

        def memset(eng, out, val):
            P.op(eng, lambda e: e.memset(out.ap, val), writes=[out])

        def dma(out, in_, key, eng="sp", slow=False):
            return P.op(eng, lambda e: e.dma_start(out=apof(out), in_=apof(in_), allow_slow_non_contiguous=slow), reads=[in_], writes=[out], dma=key)

        cst_f = T(r3(RES.f32(NCONST * 128), NCONST), "cst_f")
        cst_b = T(r3(RES.bf16(NCONST * 128), NCONST), "cst_b")
        cvec = T(RES.f32(NCV), "cvec")
        acol = T(RES.f32(1), "acol")
        dum = {e: RES.f32(1) for e in ("act", "dve", "pool")}
        ident_f = cst_f[:, 0, :]
        ident_b, perm_b, oblk_b, ones_b, le_b, gt_b, ge_b, lt_b = [cst_b[:, i, :] for i in range(8)]
        sel_f = cst_f[:, 8, :]
        dma(cst_f, consts_d, "c0")
        dma(cvec, cv_d, "c1")
        cp("dve", cst_b, cst_f)
        cvs = lambda n, i=0: cvec[:, CV[n][0] + i:CV[n][0] + i + 1]
        act(acol[0:64, :], cvs("alog")[0:64, :], AF.Exp)
        ts("dve", acol[0:64, :], acol[0:64, :], -1.0, None, ALU.mult)
        const_mark = RES.off
        KT = [T(RES.bf16(S), "KT%d" % i) for i in range(2)]
        VA_flat = T(RES.bf16(NT * 4 * 65), "VA")
        VA = VA_flat.v(lambda a: a[:, 0:NT * 4 * 65].rearrange("p (c g e) -> p c g e", c=NT, g=4))
        res_mark = RES.off

        def phase_arena(early=False):
            a = Arena(sb_t, NW_ALL)
            a.off = const_mark if early else res_mark
            return a

        for n, k, m, mw in WSPEC:
            nr = wshape(k, m, mw)[0]
            rows = 512 if nr >= 512 else nr
            for r0 in range(0, nr, rows):
                r1 = min(nr, r0 + rows)
                dma(T(wb[n][r0:r1, :], "wb_" + n), T(wsrc[n][r0:r1, :], "wsrc_" + n), "wc", eng="pool")

        class WStream:
            def __init__(self, A, nslots=3):
                self.slots = [T(A.bf16(4096), "wslot%d" % i) for i in range(nslots)]
                self.ns = nslots
                self.specs = []
                self.idx = 0
                self.loaded = 0

            def plan(self, specs):
                self.specs.extend(specs)

            def _load(self, i):
                n, kc, m0, mw = self.specs[i]
                sl = self.slots[i % self.ns]
                assert WMW[n] == mw and m0 % mw == 0, (n, mw, m0)
                mt = m0 // mw
                src = T(wb[n][mt * 128:(mt + 1) * 128, :], "wb_" + n)
                dma(sl.v(lambda a: a[:, 0:kc * mw]), src, "w%d" % (i % self.ns))

            def next(self, name):
                while self.loaded < min(len(self.specs), self.idx + self.ns):
                    self._load(self.loaded)
                    self.loaded += 1
                n, kc, m0, mw = self.specs[self.idx]
                assert n == name, (n, name)
                sl = self.slots[self.idx % self.ns]
                self.idx += 1
                return sl.v(lambda a: r3(a[:, 0:kc * mw], kc))

        def rms_cols(A_sq, src_fn, nck, pb, out_rinv, n, scratch_sd):
            for m in range(nck):
                sq = A_sq[m % 2]
                tt("pool", sq, src_fn(m), src_fn(m), ALU.mult)
                mm(PS[pb][:, 0:n], ones_b, sq, start=(m == 0), stop=(m == nck - 1))
            act(scratch_sd, PS[pb][:, 0:n], AF.Sqrt, scale=1.0 / (128 * nck), bias=EPS)
            recip(out_rinv, scratch_sd)

        P.barrier(dum)
        A = phase_arena(True)
        w1bc = T(A.f32(1024), "w1bc")
        dma(w1bc, T(rows_d[0:1, 0:1024].partition_broadcast(128), "rows"), "c2")
        zt = T(A.bf16(16), "zt")
        memset("dve", zt, 0.0)
        for d_ in (hT_d, h2T_d):
            for col in (0, S + 1):
                dma(T(d_[:, :, col:col + 1], "halo"), zt.v(lambda a: r3(a[:, 0:8], 8)), "c3", slow=True)
        xt = [T(r3(A.f32(CPT * 1024), CPT), "A_xt%d" % i) for i in range(2)]
        pt = [T(r3(A.f32(CPT * 256), CPT), "A_pt%d" % i) for i in range(2)]
        hb = T(r3(A.bf16(CPT * 1024), CPT), "A_hb")
        pbf = T(r3(A.bf16(CPT * 256), CPT), "A_pbf")
        junk = T(A.f32(1024), "A_junk")
        ss = T(A.f32(4), "A_ss")
        sd = T(A.f32(4), "A_sd")
        rstd = T(A.f32(4), "A_rstd")
        hT_sb = T(r3(A.bf16(8 * TT), 8), "A_hT")
        xT_sb = T(r3(A.f32(8 * TT), 8), "A_xT")
        pT_sb = T(r3(A.bf16(2 * TT), 2), "A_pT")

        def loadA(j):
            dma(xt[j % 2], T(x_d[j * TT:(j + 1) * TT, :].rearrange("(c p) d -> p c d", p=128), "x"), "ax%d" % (j % 2))
            dma(pt[j % 2], T(p_d[j * TT:(j + 1) * TT, :].rearrange("(c p) d -> p c d", p=128), "p"), "ap%d" % (j % 2))

        loadA(0)
        for j in range(NTT):
            if j + 1 < NTT:
                loadA(j + 1)
            X = xt[j % 2]
            memset("dve", ss, 0.0)
            for c in range(CPT):
                act(junk, X[:, c, :], AF.Square, accum=ss[:, c:c + 1])
            act(sd[:, 0:CPT], ss[:, 0:CPT], AF.Sqrt, scale=1.0 / D, bias=EPS)
            recip(rstd[:, 0:CPT], sd[:, 0:CPT])
            for c in range(CPT):
                stt("dve", hb[:, c, :], X[:, c, :], rstd[:, c:c + 1], w1bc, ALU.mult, ALU.mult)
                cp("pool", pbf[:, c, :], pt[j % 2][:, c, :])
            for c in range(CPT):
                pb = c % 2
                for kc in range(8):
                    tr(psbf(pb)[:, kc * 128:(kc + 1) * 128], hb[:, c, kc * 128:(kc + 1) * 128], ident_b)
                cp("act", hT_sb[:, :, c * 128:(c + 1) * 128], psbf(pb).v(lambda a: r3(a, 8)))
                xb = 2 + 2 * (c % 2)
                for kc in range(8):
                    tr(PS[xb + kc // 4][:, (kc % 4) * 128:(kc % 4 + 1) * 128], X[:, c, kc * 128:(kc + 1) * 128], ident_f)
                for hh in range(2):
                    cp("dve", xT_sb[:, 4 * hh:4 * hh + 4, c * 128:(c + 1) * 128], PS[xb + hh].v(lambda a: r3(a, 4)))
                for kc in range(2):
                    tr(psbf(6 + c % 2)[:, kc * 128:(kc + 1) * 128], pbf[:, c, kc * 128:(kc + 1) * 128], ident_b)
                cp("act", pT_sb[:, :, c * 128:(c + 1) * 128], psbf(6 + c % 2)[:, 0:256].v(lambda a: r3(a, 2)))
            dma(T(hT_d[:, :, 1 + j * TT:1 + (j + 1) * TT], "hT_d%d" % j), hT_sb, "ao0")
            dma(T(xT_d[:, :, j * TT:(j + 1) * TT], "xT_d%d" % j), xT_sb, "ao1")
            dma(T(pT_d[:, :, j * TT:(j + 1) * TT], "pT_d%d" % j), pT_sb, "ao2")
        P.barrier(dum)

        def conv3(acc, main, halo, w0, w1, w2, b):
            n = TT
            act(acc, main, AF.Identity, scale=w1, bias=b)
            stt("dve", acc[:, 1:n], main[:, 0:n - 1], w0, acc[:, 1:n], ALU.mult, ALU.add)
            stt("dve", acc[:, 0:n - 1], main[:, 1:n], w2, acc[:, 0:n - 1], ALU.mult, ALU.add)
            stt("dve", acc[:, 0:1], halo[:, 0:1], w0, acc[:, 0:1], ALU.mult, ALU.add)
            stt("dve", acc[:, n - 1:n], halo[:, 1:2], w2, acc[:, n - 1:n], ALU.mult, ALU.add)

        def load_hTw(dst, src_d, j, key):
            dma(dst, T(src_d[:, :, j * TT:j * TT + TT + 2], "hTsrc"), key)

        A = phase_arena(True)
        WS = WStream(A)
        hTw = [T(r3(A.bf16(8 * (TT + 2)), 8), "B_hTw%d" % i) for i in range(2)]
        wdt = T(r3(A.bf16(8 * 64), 8), "B_wdt")
        dma(wdt, T(wb["w_dt"].rearrange("(kc p) m -> p kc m", p=128), "wb_w_dt"), "c4")
        xTc = T(r3(A.bf16(16 * TT), 16), "B_xTc")
        bTc = T(r3(A.bf16(4 * TT), 4), "B_bTc")
        cTc = T(r3(A.bf16(4 * TT), 4), "B_cTc")
        acc = [T(A.f32(TT), "B_acc%d" % i) for i in range(2)]
        ddT = T(A.f32(TT), "B_ddT")
        xtok = [T(A.bf16(2048), "B_xtok%d" % i) for i in range(2)]
        btok = [T(A.bf16(512), "B_btok%d" % i) for i in range(2)]
        ddtok = [T(A.f32(128), "B_ddtok%d" % i) for i in range(2)]
        for j in range(NTT):
            WS.plan([("w_xbc", 8, g * 512, 512) for g in range(6)])
        load_hTw(hTw[0], hT_d, 0, "bh0")
        for j in range(NTT):
            if j + 1 < NTT:
                load_hTw(hTw[(j + 1) % 2], hT_d, j + 1, "bh%d" % ((j + 1) % 2))
            H = hTw[j % 2]
            for g in range(6):
                wt = WS.next("w_xbc")
                for mi in range(4):
                    m = g * 4 + mi
                    pb = m % 2
                    for kc in range(8):
                        mm(PS[pb][:, 0:TT], wt[:, kc, mi * 128:(mi + 1) * 128], H[:, kc, 1:TT + 1], start=(kc == 0), stop=(kc == 7))
                    for kc in range(8):
                        mm(PS[2 + pb][:, 0:2], wt[:, kc, mi * 128:(mi + 1) * 128], H[:, kc, 0:TT + 2:TT + 1], start=(kc == 0), stop=(kc == 7))
                    ac = acc[m % 2]
                    conv3(ac, PS[pb][:, 0:TT], PS[2 + pb][:, 0:2], cvs("cw0", m), cvs("cw1", m), cvs("cw2", m), cvs("cb", m))
                    dst = xTc[:, m, :] if m < 16 else (bTc[:, m - 16, :] if m < 20 else cTc[:, m - 20, :])
                    act(dst, ac, AF.Silu)
            for kc in range(8):
                mm(PS[4][0:64, 0:TT], wdt[:, kc, :], H[:, kc, 1:TT + 1], start=(kc == 0), stop=(kc == 7))
            act(ddT[0:64, :], PS[4][0:64, 0:TT], AF.Exp, bias=cvs("dtb")[0:64, :])
            act(ddT[0:64, :], ddT[0:64, :], AF.Ln, bias=1.0)
            ts("dve", ddT[64:128, :], ddT[0:64, :], acol[0:64, :], None, ALU.mult)
            dma(T(bT_d[:, :, j * TT:(j + 1) * TT], "bT_d%d" % j), bTc, "bo0")
            dma(T(cT_d[:, :, j * TT:(j + 1) * TT], "cT_d%d" % j), cTc, "bo1")
            for c in range(CPT):
                cg = j * CPT + c
                q = c % 2
                for m in range(16):
                    tr(psbf(5 + m // 8)[:, (m % 8) * 128:(m % 8 + 1) * 128], xTc[:, m, c * 128:(c + 1) * 128], ident_b)
                cp("act", xtok[q][:, 0:1024], psbf(5))
                cp("dve", xtok[q][:, 1024:2048], psbf(6))
                for m in range(4):
                    tr(psbf(7)[:, m * 128:(m + 1) * 128], bTc[:, m, c * 128:(c + 1) * 128], ident_b)
                cp("act", btok[q], psbf(7)[:, 0:512])
                tr(PS[4][:, 0:128], ddT[:, c * 128:(c + 1) * 128], ident_f)
                cp("dve", ddtok[q], PS[4][:, 0:128])
                dma(T(xtok_d[cg * 128:(cg + 1) * 128, :], "xtok_d%d" % cg), xtok[q], "bo2%d" % q)
                dma(T(btok_d[cg * 128:(cg + 1) * 128, :], "btok_d%d" % cg), btok[q], "bo3%d" % q)
                dma(T(dd_d[cg * 128:(cg + 1) * 128, :], "dd_d%d" % cg), ddtok[q], "bo4%d" % q)
        P.barrier(dum)

        A = phase_arena(True)
        s_x = [T(A.bf16(2048), "S_x%d" % i) for i in range(2)]
        s_b = [T(A.bf16(512), "S_b%d" % i) for i in range(2)]
        s_bT = [T(r3(A.bf16(512), 4), "S_bT%d" % i) for i in range(2)]
        s_cT = [T(r3(A.bf16(512), 4), "S_cT%d" % i) for i in range(2)]
        s_dd = [T(A.f32(128), "S_dd%d" % i) for i in range(2)]
        s_adtb = T(A.bf16(32), "S_adtb")
        s_E = T(A.f32(96), "S_E")
        s_xdt = T(A.bf16(2048), "S_xdt")
        s_xdtA = T(A.bf16(2048), "S_xdtA")
        s_at = [T(A.bf16(512), "S_at%d" % i) for i in range(3)]
        s_dec = [T(A.bf16(512), "S_dec%d" % i) for i in range(3)]
        s_M = [T(A.bf16(512), "S_M%d" % i) for i in range(3)]
        s_cbm = T(A.bf16(512), "S_cbm")
        _sh = A.f32(2048)
        _shb = A.bf16(2048)
        s_hg = [T(_sh[:, g * 512:(g + 1) * 512], "S_h%d" % g) for g in range(4)]
        s_hbg = [T(_shb[:, g * 512:(g + 1) * 512], "S_hb%d" % g) for g in range(4)]
        s_tmp = T(A.f32(512), "S_tmp")
        s_y = [T(A.f32(2048), "S_y%d" % i) for i in range(2)]
        s_wz = T(r3(A.bf16(8 * 2048), 8), "S_wz")
        s_hT = [T(r3(A.bf16(8 * 128), 8), "S_hT%d" % i) for i in range(3)]
        s_yf = [T(A.f32(2048), "S_yf%d" % i) for i in range(2)]
        s_nw = T(A.f32(2048), "S_nw")
        s_dk = T(A.f32(32), "S_dk")
        s_sz = [T(A.f32(512), "S_sz%d" % i) for i in range(2)]
        s_zs = [T(A.f32(512), "S_zs%d" % i) for i in range(2)]
        s_gs = T(A.f32(4), "S_gs")
        s_gsd = T(A.f32(4), "S_gsd")
        s_gr = T(A.f32(4), "S_gr")
        s_junk = T(A.f32(512), "S_junk")
        s_sb = T(A.bf16(2048), "S_sb")
        s_sT = T(r3(A.bf16(16 * 128), 16), "S_sT")
        dma(s_wz, T(wb["w_z"].rearrange("(kc p) m -> p kc m", p=128), "wb_w_z"), "c5")
        dma(s_nw, T(rows_d[0:1, 1024:3072].partition_broadcast(128), "rows"), "c6")
        dma(s_dk, T(rows_d[0:1, 3072:3104].partition_broadcast(128), "rows"), "c7")
        bc_h = lambda t, n: t.v(lambda a: a.unsqueeze(2).to_broadcast([128, n, 64]))
        s_did = T(r3(A.bf16(32 * 128), 32), "S_did")
        for h in range(32):
            ts("dve", s_did[:, h, :], ident_b, s_dk[:, h:h + 1], None, ALU.mult)

        def ssd_load(c, q, di, t3=0):
            dma(s_x[q], T(xtok_d[c * 128:(c + 1) * 128, :], "xtok_d%d" % c), "sl0%d" % q)
            dma(s_b[q], T(btok_d[c * 128:(c + 1) * 128, :], "btok_d%d" % c), "sl1%d" % q)
            dma(s_bT[q], T(bT_d[:, :, c * 128:(c + 1) * 128], "bT_d%d" % (c // CPT)), "sl2%d" % q)
            dma(s_cT[q], T(cT_d[:, :, c * 128:(c + 1) * 128], "cT_d%d" % (c // CPT)), "sl3%d" % q)
            dma(s_dd[q], T(dd_d[c * 128:(c + 1) * 128, :], "dd_d%d" % c), "sl4%d" % q)
            if di == 1:
                dma(s_yf[q], T(yf_d[c * 128:(c + 1) * 128, :], "yf_d%d" % c), "sl5%d" % q)
                dma(s_hT[t3], T(hT_d[:, :, 1 + c * 128:1 + (c + 1) * 128], "hT_d%d" % (c // CPT)), "sl6%d" % t3)

        for di in range(2):
            tri, U, mask = (le_b, gt_b, le_b) if di == 0 else (ge_b, lt_b, ge_b)
            order = list(range(NT)) if di == 0 else list(range(NT - 1, -1, -1))
            for g in range(4):
                memset("dve", s_hg[g], 0.0)
                memset("pool", s_hbg[g], 0.0)
            ssd_load(order[0], 0, di, 0)
            pend_tail = []
            for it, c in enumerate(order):
                q = it % 2
                if it + 1 < NT:
                    ssd_load(order[it + 1], (it + 1) % 2, di, (it + 1) % 3)
                dtc = s_dd[q][:, di * 32:(di + 1) * 32]
                adt = s_dd[q][:, 64 + di * 32:64 + (di + 1) * 32]
                cp("pool", s_adtb, adt)
                mm(PS[0][:, 0:32], tri, s_adtb)
                mm(PS[0][:, 32:64], U, s_adtb)
                mm(PS[0][:, 64:96], ones_b, s_adtb)
                act(s_E, PS[0][:, 0:96], AF.Exp)
                eac, dA, cd = s_E[:, 0:32], s_E[:, 32:64], s_E[:, 64:96]
                X3 = s_x[q].v(lambda a: r3(a, 32))
                for g in range(4):
                    mm(PS[1][:, g * 128:(g + 1) * 128], s_bT[q][:, g, :], s_cT[q][:, g, :])
                tt("dve", s_cbm.v(lambda a: r3(a, 4)), PS[1].v(lambda a: r3(a, 4)),
                   mask.v(lambda a: a.unsqueeze(1).to_broadcast([128, 4, 128])), ALU.mult)
                Y = s_y[q]
                hgs = [(g, hh) for g in range(4) for hh in range(2)]

                SEGB = (2, 3, 7)

                def stage1(i):
                    g, hh = hgs[i]
                    k = i % 3
                    h0 = g * 8 + hh * 4
                    tt("pool" if i % 2 == 0 else "dve", s_at[k].v(lambda a: r3(a, 4)),
                       tri.v(lambda a: a.unsqueeze(1).to_broadcast([128, 4, 128])),
                       adt[:, h0:h0 + 4].v(lambda a: a.unsqueeze(2).to_broadcast([128, 4, 128])), ALU.mult)
                    mm(PS[SEGB[k]], U, s_at[k])
                    act(s_dec[k], PS[SEGB[k]], AF.Exp)
                    tt("dve", s_M[k].v(lambda a: r3(a, 4)), s_dec[k].v(lambda a: r3(a, 4)),
                       s_cbm[:, g * 128:(g + 1) * 128].v(lambda a: a.unsqueeze(1).to_broadcast([128, 4, 128])), ALU.mult)

                def stage2(i):
                    g, hh = hgs[i]
                    k = i % 3
                    h0 = g * 8 + hh * 4
                    for hi in range(4):
                        h = h0 + hi
                        oy = PS[4 + g % 2][:, (hh * 4 + hi) * 64:(hh * 4 + hi + 1) * 64]
                        mm(oy, s_M[k][:, hi * 128:(hi + 1) * 128], s_xdt[:, h * 64:(h + 1) * 64], start=True, stop=(di == 0))
                        if di == 1:
                            mm(oy, s_did[:, h, :], s_x[q][:, h * 64:(h + 1) * 64], start=False, stop=True)
                    if hh == 1:
                        mm(PS[6], s_cT[q][:, g, :], s_hbg[g])
                        tt("dve", s_tmp.v(lambda a: r3(a, 8)), PS[6].v(lambda a: r3(a, 8)), bc_h(eac[:, g * 8:(g + 1) * 8], 8), ALU.mult)
                        if di == 1:
                            tt("dve", s_tmp, s_tmp, s_yf[q][:, g * 512:(g + 1) * 512], ALU.add)
                        tt("dve", Y[:, g * 512:(g + 1) * 512], PS[4 + g % 2], s_tmp, ALU.add)

                stage1(0)
                stage1(1)
                for g_ in range(4):
                    tt("pool", s_xdt[:, g_ * 512:(g_ + 1) * 512].v(lambda a: r3(a, 8)), s_x[q][:, g_ * 512:(g_ + 1) * 512].v(lambda a: r3(a, 8)),
                       bc_h(dtc[:, g_ * 8:(g_ + 1) * 8], 8), ALU.mult)
                for i in range(8):
                    if i + 2 < 8:
                        stage1(i + 2)
                    stage2(i)
                    if pend_tail:
                        for f_ in pend_tail.pop(0):
                            f_()
                    if i in (1, 2, 3, 4):
                        g_ = i - 1
                        tt("pool", s_xdtA[:, g_ * 512:(g_ + 1) * 512].v(lambda a: r3(a, 8)), s_xdt[:, g_ * 512:(g_ + 1) * 512].v(lambda a: r3(a, 8)),
                           bc_h(dA[:, g_ * 8:(g_ + 1) * 8], 8), ALU.mult)
                    if i in (4, 5, 6, 7):
                        g_ = i - 4
                        tt("pool", s_hg[g_].v(lambda a: r3(a, 8)), s_hg[g_].v(lambda a: r3(a, 8)),
                           bc_h(cd[:, g_ * 8:(g_ + 1) * 8], 8), ALU.mult)
                for g in range(4):
                    mm(PS[g % 2], s_b[q][:, g * 128:(g + 1) * 128], s_xdtA[:, g * 512:(g + 1) * 512])
                    tt("dve", s_hg[g], PS[g % 2], s_hg[g], ALU.add)
                    cp("act", s_hbg[g], s_hg[g])
                if di == 0:
                    dma(T(yf_d[c * 128:(c + 1) * 128, :], "yf_d%d" % c), Y, "so0%d" % q)
                else:
                    while pend_tail:
                        for f_ in pend_tail.pop(0):
                            f_()
                    def make_tail(c=c, Y=Y, hT=s_hT[it % 3]):
                        def z_mm(g):
                            for kc in range(8):
                                mm(PS[g % 2], hT[:, kc, :], s_wz[:, kc, g * 512:(g + 1) * 512], start=(kc == 0), stop=(kc == 7))
                        def f_silu(g):
                            act(s_sz[g % 2], PS[g % 2], AF.Tanh, scale=0.5)
                            act(s_zs[g % 2], PS[g % 2], AF.Identity, scale=0.5)

                        def f_mul(g):
                            Yg = Y[:, g * 512:(g + 1) * 512]
                            stt("dve", Yg, s_sz[g % 2], 1.0, Yg, ALU.add, ALU.mult)
                            tt("pool", Yg, Yg, s_zs[g % 2], ALU.mult)
                        f_sq = lambda g: act(s_junk, Y[:, g * 512:(g + 1) * 512], AF.Square, accum=s_gs[:, g:g + 1])
                        slots = []
                        for t in range(7):
                            sl = []
                            if t == 0:
                                sl.append(lambda: memset("dve", s_gs, 0.0))
                            for fn_, off in ((z_mm, 0), (f_silu, 1), (f_mul, 2), (f_sq, 3)):
                                g = t - off
                                if 0 <= g < 4:
                                    sl.append(lambda fn_=fn_, g=g: fn_(g))
                            slots.append(sl)
                        slots[6].append(lambda: act(s_gsd, s_gs, AF.Sqrt, scale=1.0 / 512, bias=EPS))

                        def fin_a():
                            recip(s_gr, s_gsd)
                            for g in range(4):
                                stt("dve", s_sb[:, g * 512:(g + 1) * 512], Y[:, g * 512:(g + 1) * 512], s_gr[:, g:g + 1],
                                    s_nw[:, g * 512:(g + 1) * 512], ALU.mult, ALU.mult)

                        def fin_b():
                            for m in range(16):
                                tr(psbf(0 if m < 8 else 1)[:, (m % 8) * 128:(m % 8 + 1) * 128], s_sb[:, m * 128:(m + 1) * 128], ident_b)
                            cp("act", s_sT[:, 0:8, :], psbf(0).v(lambda a: r3(a, 8)))
                            cp("dve", s_sT[:, 8:16, :], psbf(1).v(lambda a: r3(a, 8)))
                            dma(T(ssdT_d[:, :, c * 128:(c + 1) * 128], "ssdT_d%d" % (c // CPT)), s_sT, "so1")
                        slots.append([fin_a])
                        slots.append([fin_b])
                        return slots
                    pend_tail = make_tail()
            while pend_tail:
                for f_ in pend_tail.pop(0):
                    f_()
            P.barrier(dum)

        A = phase_arena()
        hTw = [T(r3(A.bf16(8 * (TT + 2)), 8), "K_hTw%d" % i) for i in range(2)]
        wk = T(r3(A.bf16(8 * 256), 8), "K_wk")
        wv = T(r3(A.bf16(8 * 256), 8), "K_wv")
        dma(wk, T(wb["w_k"].rearrange("(kc p) m -> p kc m", p=128), "wb_w_k"), "c8")
        dma(wv, T(wb["w_v"].rearrange("(kc p) m -> p kc m", p=128), "wb_w_v"), "c9")
        cs = [[T(A.f32(TT), "K_cs%d%d" % (i, k)) for k in range(2)] for i in range(2)]
        memset("dve", VA_flat, 1.0)

        class Rope:
            def __init__(self, A, pfx):
                self.sq = T(A.bf16(TT), pfx + "sq")
                self.sd = T(A.f32(TT), pfx + "sd")
                self.ri = T(A.f32(TT), pfx + "ri")
                self.kn = T(A.bf16(TT), pfx + "kn")
                self.t1 = T(A.f32(TT), pfx + "t1")
                self.t2 = T(A.f32(TT), pfx + "t2")

            def run(self, src_ps, wcol, cosT, sinT, dst, pbs, pbs2):
                act(self.sq, src_ps, AF.Square)
                act(self.kn, src_ps, AF.Identity, scale=wcol)
                mm(PS[pbs][:, 0:TT], oblk_b, self.sq)
                mm(PS[pbs2][:, 0:TT], perm_b, self.kn)
                act(self.sd, PS[pbs][:, 0:TT], AF.Sqrt, scale=1.0 / 64, bias=EPS)
                tt("pool", self.t1, self.kn, cosT, ALU.mult)
                tt("dve", self.t2, PS[pbs2][:, 0:TT], sinT, ALU.mult)
                recip(self.ri, self.sd)
                tt("pool", self.t1, self.t1, self.t2, ALU.add)
                tt("dve", dst, self.t1, self.ri, ALU.mult)

        RP = Rope(A, "K_")

        def load_cs(dst, j, key):
            dma(dst[0], T(cos_d[:, j * TT:(j + 1) * TT], "cos"), key + "a")
            dma(dst[1], T(sin_d[:, j * TT:(j + 1) * TT], "sin"), key + "b")

        load_hTw(hTw[0], hT_d, 0, "kh0")
        load_cs(cs[0], 0, "kc0")
        for j in range(NTT):
            if j + 1 < NTT:
                load_hTw(hTw[(j + 1) % 2], hT_d, j + 1, "kh%d" % ((j + 1) % 2))
                load_cs(cs[(j + 1) % 2], j + 1, "kc%d" % ((j + 1) % 2))
            H = hTw[j % 2]
            for pp in range(2):
                for kc in range(8):
                    mm(PS[pp][:, 0:TT], wk[:, kc, pp * 128:(pp + 1) * 128], H[:, kc, 1:TT + 1], start=(kc == 0), stop=(kc == 7))
                RP.run(PS[pp][:, 0:TT], cvs("kw"), cs[j % 2][0], cs[j % 2][1], KT[pp][:, j * TT:(j + 1) * TT], 2 + pp, 4 + pp)
            for c in range(CPT):
                cg = j * CPT + c
                pb = 6 + c % 2
                for kc in range(8):
                    mm(PS[pb][:, 0:256], H[:, kc, 1 + c * 128:1 + (c + 1) * 128], wv[:, kc, :], start=(kc == 0), stop=(kc == 7))
                cp("act" if c % 2 == 0 else "dve", VA[:, cg, :, 0:64], PS[pb][:, 0:256].v(lambda a: r3(a, 4)))
        P.barrier(dum)

        A = phase_arena()
        WS = WStream(A)
        hT1 = T(r3(A.bf16(8 * TT), 8), "E_hT")
        cs1 = [T(A.f32(TT), "E_cs%d" % k) for k in range(2)]
        QT = T(r3(A.bf16(8 * TT), 8), "E_QT")
        attnT = T(r3(A.bf16(8 * TT), 8), "E_attnT")
        PT = [T(A.bf16(2 * TT), "E_PT%d" % i) for i in range(2)]
        ssdT = T(r3(A.bf16(16 * TT), 16), "E_ssdT")
        xT = T(r3(A.f32(8 * TT), 8), "E_xT")
        RQ = Rope(A, "E_")
        RQ2 = [RQ, Rope(A, "E2_")]
        o_sb = T(A.f32(TT), "E_osb")
        o_rec = T(A.f32(TT), "E_orec")
        ga, gs_, m1, m2 = RQ.sd, RQ.ri, RQ.t1, RQ.t2
        sqb = [T(A.bf16(TT), "E_sqb%d" % i) for i in range(2)]
        rinv, sdt = o_sb, o_rec
        scale = 64 ** -0.5
        for j in range(NTT):
            WS.plan([("w_q", 8, g * 512, 512) for g in range(2)])
            for m in range(8):
                WS.plan([("w_a", 8, m * 128, 128), ("w_s", 16, m * 128, 128), ("w_g", 8, m * 128, 128), ("w_g", 8, 1024 + m * 128, 128)])
            WS.plan([("w_o", 8, g * 512, 512) for g in range(2)])
        def loadE_early(j):
            dma(hT1, T(hT_d[:, :, 1 + j * TT:1 + (j + 1) * TT], "hT_d%d" % j), "eh")
            load_cs(cs1, j, "ec")
            dma(ssdT, T(ssdT_d[:, :, j * TT:(j + 1) * TT], "ssdT_d%d" % j), "es")

        loadE_early(0)
        for j in range(NTT):
            dma(xT, T(xT_d[:, :, j * TT:(j + 1) * TT], "xT_d%d" % j), "ex")
            for g in range(2):
                wt = WS.next("w_q")
                for mi in range(4):
                    m = g * 4 + mi
                    pb = m % 2
                    for kc in range(8):
                        mm(PS[pb][:, 0:TT], wt[:, kc, mi * 128:(mi + 1) * 128], hT1[:, kc, :], start=(kc == 0), stop=(kc == 7))
                    RQ2[m % 2].run(PS[pb][:, 0:TT], cvs("qw"), cs1[0], cs1[1], QT[:, m, :], 2 + pb, 4 + pb)
            iters = [(pp, r, kb) for pp in range(2) for r in range(4) for kb in range(NT)]
            bcs = [[o_sb, RQ.t1], [RQ2[1].t1, RQ2[1].t2]]
            orec = [o_rec, RQ.t2]
            npair = [0]

            def emit_qk(i):
                pp, r, kb = iters[i]
                jq = pp * 4 + r
                sb_ = (i % 2) * 2
                pt_ = PT[i % 2]
                mm(PS[sb_][:, 0:TT], KT[pp][0:64, kb * 128:(kb + 1) * 128], QT[0:64, jq, :])
                mm(PS[sb_ + 1][:, 0:TT], KT[pp][64:128, kb * 128:(kb + 1) * 128], QT[64:128, jq, :])
                P.op("act", lambda e, o=pt_.ap, i_=ps_t[:, sb_ * 512:sb_ * 512 + 2 * 512]: e.activation(out=o, in_=i_, func=AF.Exp, scale=scale),
                     reads=[PS[sb_], PS[sb_ + 1]], writes=[pt_])

            emit_qk(0)
            for i in range(len(iters)):
                if i + 1 < len(iters):
                    emit_qk(i + 1)
                pp, r, kb = iters[i]
                pt_ = PT[i % 2]
                par = (i // NT) % 2
                ob = 4 + 2 * par
                mm(PS[ob][0:65, 0:TT], VA[:, kb, 2 * pp, :], pt_[:, 0:TT], start=(kb == 0), stop=(kb == NT - 1))
                mm(PS[ob + 1][0:65, 0:TT], VA[:, kb, 2 * pp + 1, :], pt_[:, TT:2 * TT], start=(kb == 0), stop=(kb == NT - 1))
                if kb == NT - 1:
                    jq = pp * 4 + r
                    for half in range(2):
                        rr = orec[par][64:65, :] if half == 0 else orec[par][32:33, :]
                        recip(rr, PS[ob + half][64:65, 0:TT])
                        rsd = T(rs_d[2 * par + half:2 * par + half + 1, 0:TT], "rs_d%d%d" % (par, half))
                        dma(rsd, rr, "rs%d%d" % (par, half))
                        dma(bcs[par][half][0:64, :], rsd.v(lambda a: a.partition_broadcast(64)), "rb%d%d" % (par, half))
                    for half in range(2):
                        tt("dve", attnT[half * 64:(half + 1) * 64, jq, :], PS[ob + half][0:64, 0:TT], bcs[par][half][0:64, :], ALU.mult)
            MT = QT
            for m in range(8):
                wa = WS.next("w_a")
                for kc in range(8):
                    mm(PS[0][:, 0:TT], wa[:, kc, :], attnT[:, kc, :], start=(kc == 0), stop=(kc == 7))
                wsd = WS.next("w_s")
                for kc in range(16):
                    mm(PS[1][:, 0:TT], wsd[:, kc, :], ssdT[:, kc, :], start=(kc == 0), stop=(kc == 15))
                wg1 = WS.next("w_g")
                for kc in range(8):
                    mm(PS[2][:, 0:TT], wg1[:, kc, :], hT1[:, kc, :], start=(kc == 0), stop=(kc == 7))
                wg2 = WS.next("w_g")
                for kc in range(8):
                    mm(PS[3][:, 0:TT], wg2[:, kc, :], hT1[:, kc, :], start=(kc == 0), stop=(kc == 7))
                act(ga, PS[2][:, 0:TT], AF.Sigmoid, bias=cvs("gb", m))
                act(gs_, PS[3][:, 0:TT], AF.Sigmoid, bias=cvs("gb", 8 + m))
                tt("dve", m1, PS[0][:, 0:TT], ga, ALU.mult)
                tt("dve", m2, PS[1][:, 0:TT], gs_, ALU.mult)
                tt("pool", MT[:, m, :], m1, m2, ALU.add)
            if j + 1 < NTT:
                loadE_early(j + 1)
            for g in range(2):
                wo = WS.next("w_o")
                for mi in range(4):
                    m = g * 4 + mi
                    pb = m % 2
                    for kc in range(8):
                        mm(PS[pb][:, 0:TT], wo[:, kc, mi * 128:(mi + 1) * 128], MT[:, kc, :], start=(kc == 0), stop=(kc == 7))
                    tt("dve", xT[:, m, :], PS[pb][:, 0:TT], xT[:, m, :], ALU.add)
            dma(T(x1T_d[:, :, j * TT:(j + 1) * TT], "x1T_d%d" % j), xT, "eo0")
            rms_cols(sqb, lambda m: xT[:, m, :], 8, 7, rinv, TT, sdt)
            H2 = attnT
            for m in range(8):
                stt("dve", H2[:, m, :], xT[:, m, :], cvs("n2", m), rinv, ALU.mult, ALU.mult)
            dma(T(h2T_d[:, :, 1 + j * TT:1 + (j + 1) * TT], "h2T_d%d" % j), H2, "eo1")
        P.barrier(dum)

        A = phase_arena()
        WS = WStream(A)
        h2w = T(r3(A.bf16(8 * (TT + 2)), 8), "F_h2w")
        x1 = T(r3(A.f32(8 * TT), 8), "F_x1")
        pT = T(r3(A.bf16(2 * TT), 2), "F_pT")
        _g = A.f32(11 * TT)
        gT = T(r3(_g.bitcast(BF16), 22), "F_gT")
        yT = T(r3(_g[:, 0:8 * TT], 8), "F_gT")
        accg = T(A.f32(TT), "F_accg")
        accv = T(A.f32(TT), "F_accv")
        gl = T(A.f32(TT), "F_gl")
        x2b = T(r3(A.bf16(8 * TT), 8), "F_x2b")
        sg = T(A.f32(TT), "F_sg")
        sqb = [T(A.bf16(TT), "F_sqb%d" % i) for i in range(2)]
        rinv = T(A.f32(TT), "F_rinv")
        sdt = T(A.f32(TT), "F_sdt")
        yo = [T(A.f32(1024), "F_yo%d" % i) for i in range(2)]
        for j in range(NTT):
            for m in range(22):
                WS.plan([("w_up", 8, m * 128, 128), ("w_up", 8, 2816 + m * 128, 128)])
            WS.plan([("w_dn", 22, m * 128, 128) for m in range(8)])
            for m in range(8):
                WS.plan([("w_pg", 8, m * 128, 128), ("w_pl", 2, m * 128, 128)])
        load_hTw(h2w, h2T_d, 0, "fh")
        dma(pT, T(pT_d[:, :, 0:TT], "pT_d0"), "fp")
        for j in range(NTT):
            dma(x1, T(x1T_d[:, :, j * TT:(j + 1) * TT], "x1T_d%d" % j), "fx")
            for m in range(22):
                for half, accx in ((0, accg), (1, accv)):
                    wt = WS.next("w_up")
                    mc = m + 22 * half
                    pb = half
                    for kc in range(8):
                        mm(PS[pb][:, 0:TT], wt[:, kc, :], h2w[:, kc, 1:TT + 1], start=(kc == 0), stop=(kc == 7))
                    for kc in range(8):
                        mm(PS[2 + pb][:, 0:2], wt[:, kc, :], h2w[:, kc, 0:TT + 2:TT + 1], start=(kc == 0), stop=(kc == 7))
                    conv3(accx, PS[pb][:, 0:TT], PS[2 + pb][:, 0:2], cvs("fw0", mc), cvs("fw1", mc), cvs("fw2", mc), cvs("fb", mc))
                act(gl, accg, AF.Gelu_apprx_tanh)
                tt("pool", gT[:, m, :], gl, accv, ALU.mult)
            if j + 1 < NTT:
                load_hTw(h2w, h2T_d, j + 1, "fh")
            for m in range(8):
                wt = WS.next("w_dn")
                for kc in range(22):
                    mm(PS[4 + m % 2][:, 0:TT], wt[:, kc, :], gT[:, kc, :], start=(kc == 0), stop=(kc == 21))
                tt("dve", x1[:, m, :], PS[4 + m % 2][:, 0:TT], x1[:, m, :], ALU.add)
                cp("pool", x2b[:, m, :], x1[:, m, :])
            for m in range(8):
                wt = WS.next("w_pg")
                for kc in range(8):
                    mm(PS[6][:, 0:TT], wt[:, kc, :], x2b[:, kc, :], start=(kc == 0), stop=(kc == 7))
                wt = WS.next("w_pl")
                for kc in range(2):
                    mm(PS[7][:, 0:TT], wt[:, kc, :], pT[:, kc, :], start=(kc == 0), stop=(kc == 1))
                act(sg, PS[6][:, 0:TT], AF.Sigmoid, bias=cvs("bpg", m))
                tt("dve", sg, PS[7][:, 0:TT], sg, ALU.mult)
                tt("pool", x1[:, m, :], x1[:, m, :], sg, ALU.add)
            if j + 1 < NTT:
                dma(pT, T(pT_d[:, :, (j + 1) * TT:(j + 2) * TT], "pT_d%d" % (j + 1)), "fp")
            rms_cols(sqb, lambda m: x1[:, m, :], 8, 0, rinv, TT, sdt)
            for m in range(8):
                stt("dve", yT[:, m, :], x1[:, m, :], cvs("nf", m), rinv, ALU.mult, ALU.mult)
            for c in range(CPT):
                cg = j * CPT + c
                q = c % 2
                b0 = 1 + 2 * q
                for m in range(8):
                    tr(PS[b0 + m // 4][:, (m % 4) * 128:(m % 4 + 1) * 128], yT[:, m, c * 128:(c + 1) * 128], ident_f)
                cp("act", yo[q][:, 0:512], PS[b0])
                cp("dve", yo[q][:, 512:1024], PS[b0 + 1])
                dma(T(y_d[cg * 128:(cg + 1) * 128, :], "y_d%d" % cg), yo[q], "fo%d" % q)
        P.emit()
    return nc, P


def _consts():
    c = np.zeros((128, NCONST, 128), np.float32)
    i = np.arange(128)
    c[:, 0, :] = np.eye(128)
    pm = np.zeros((128, 128), np.float32)
    pm[i, i ^ 1] = 1.0
    c[:, 1, :] = pm
    c[:, 2, :] = (i[:, None] // 64 == i[None, :] // 64)
    c[:, 3, :] = 1.0
    c[:, 4, :] = i[:, None] <= i[None, :]
    c[:, 5, :] = i[:, None] > i[None, :]
    c[:, 6, :] = i[:, None] >= i[None, :]
    c[:, 7, :] = i[:, None] < i[None, :]
    c[64, 8, :] = 1.0
    return c


def _rope_tables(S):
    rows = S // 64
    row_idx = np.repeat(np.arange(rows, dtype=np.float32), 64)
    col_idx = np.tile(np.arange(64, dtype=np.float32), rows)
    inv_freq = (np.float32(10000.0) ** (-np.arange(0, 32, 2, dtype=np.float32) / np.float32(32))).astype(np.float32)
    ang = np.concatenate([row_idx[:, None] * inv_freq, col_idx[:, None] * inv_freq], axis=-1).astype(np.float32)
    cos, sin = np.cos(ang), np.sin(ang)
    pidx = (np.arange(128) % 64) // 2
    sign = np.where(np.arange(128) % 2 == 0, -1.0, 1.0).astype(np.float32)
    cosT = np.ascontiguousarray(cos[:, pidx].T).astype(np.float32)
    sinT = np.ascontiguousarray((sin[:, pidx] * sign[None, :]).T).astype(np.float32)
    return cosT, sinT


def _pack_weights(w):
    win = w["w_in"][0]
    o = {}
    q = win[:, 0:1024]
    qperm = []
    for pp in range(2):
        for r in range(4):
            for half in range(2):
                hq = (2 * pp + half) * 4 + r
                qperm.extend(range(hq * 64, hq * 64 + 64))
    qperm = np.array(qperm)
    o["w_q"] = q[:, qperm]
    o["w_k"] = win[:, 1024:1280]
    o["w_v"] = win[:, 1280:1536]
    o["w_z"] = win[:, 1536:3584]
    o["w_xbc"] = win[:, 3584:6656]
    o["w_dt"] = win[:, 6656:6720]
    o["w_g"] = win[:, 6720:8768]
    o["w_a"] = w["w_attn_branch"][0][qperm, :]
    o["w_s"] = w["w_ssd_branch"][0]
    o["w_o"] = w["w_out"][0]
    o["w_up"] = w["w_up"][0]
    o["w_dn"] = w["w_down"][0]
    o["w_pl"] = w["w_ple"][0]
    o["w_pg"] = w["w_ple_gate"][0]
    for n, k, m, mw in WSPEC:
        if mw is not None:
            o[n] = o[n].reshape(k // 128, 128, m // mw, mw).transpose(2, 1, 0, 3).reshape((m // mw) * 128, (k // 128) * mw)
    o = {k: np.ascontiguousarray(v, dtype=np.float32) for k, v in o.items()}
    cv = np.zeros((128, NCV), np.float32)

    def put(name, vec):
        off, wd = CV[name]
        v = np.asarray(vec, np.float32).reshape(-1)
        if v.size == 64 * wd and wd == 1:
            if name in ("qw", "kw"):
                cv[:, off] = np.concatenate([v, v])
            else:
                cv[0:64, off] = v
        else:
            cv[:, off:off + wd] = v.reshape(wd, 128).T

    put("n1", w["norm1_w"][0]); put("n2", w["norm2_w"][0]); put("nf", w["final_norm_w"])
    put("gb", w["gate_b"][0]); put("bpg", w["b_ple_gate"][0])
    put("qw", w["q_norm_w"][0]); put("kw", w["k_norm_w"][0])
    for t in range(3):
        put("cw%d" % t, w["ssm_conv_w"][0][t]); put("fw%d" % t, w["ffn_conv_w"][0][t])
    put("cb", w["ssm_conv_b"][0]); put("fb", w["ffn_conv_b"][0])
    put("dtb", w["dt_bias"][0].reshape(-1)); put("alog", w["a_log"][0].reshape(-1))
    o["cvec"] = cv
    o["rows"] = np.concatenate([w["norm1_w"][0], w["ssd_norm_w"][0], w["d_skip"][0]]).astype(np.float32)[None, :]
    o["consts"] = _consts()
    return o


_CACHE = {}


def run_seqs(xs, ps, w, S, debug=False, n_cores=None):
    if (S, debug) not in _CACHE:
        _CACHE[(S, debug)] = build(S, debug)
    nc, _ = _CACHE[(S, debug)]
    base = _pack_weights(w)
    base["cosT"], base["sinT"] = _rope_tables(S)
    in_maps = []
    for x, p in zip(xs, ps):
        m = dict(base)
        m["x"] = np.ascontiguousarray(x, dtype=np.float32)
        m["p"] = np.ascontiguousarray(p, dtype=np.float32)
        in_maps.append(m)
    res = run_bass_kernel_spmd(nc, in_maps, core_ids=list(range(len(in_maps))))
    return res.results


def kernel(**inputs):
    inp = {k: np.asarray(v) for k, v in inputs.items()}
    S = inp["x_prompt"].shape[1]
    xs = [inp["x_prompt"][b] for b in range(4)] + [inp["x_sample"][b] for b in range(2)]
    ps = [inp["p_prompt"][0, b] for b in range(4)] + [inp["p_sample"][0, b] for b in range(2)]
    xs += [xs[0], xs[1]]
    ps += [ps[0], ps[1]]
    res = run_seqs(xs, ps, inp, S)
    y_prompt = np.stack([res[b]["y"] for b in range(4)]).astype(np.float32)
    y_sample = np.stack([res[4 + b]["y"] for b in range(2)]).astype(np.float32)
    return (y_prompt, y_sample)
```

```python
import numpy as np
from contextlib import ExitStack
import concourse.bass as bass
import concourse.mybir as mybir
from concourse.bass_utils import run_bass_kernel_spmd
from concourse.alu_op_type import AluOpType as ALU

F32 = mybir.dt.float32
BF16 = mybir.dt.bfloat16
AF = mybir.ActivationFunctionType
ENGS = ("pe", "act", "dve", "pool", "sp")
EPS = 1e-6
D = 1024


class Buf:
    __slots__ = ("name", "w", "r")

    def __init__(self, name):
        self.name = name
        self.w = None
        self.r = []


class Op:
    __slots__ = ("eng", "fn", "deps", "sig", "semkey", "val", "is_dma")

    def __init__(self, eng, fn, is_dma, semkey):
        self.eng = eng
        self.fn = fn
        self.deps = []
        self.sig = False
        self.semkey = semkey
        self.val = None
        self.is_dma = is_dma


class T:
    __slots__ = ("ap", "buf")

    def __init__(self, ap, buf):
        self.ap = ap
        self.buf = buf

    def __getitem__(self, idx):
        return T(self.ap[idx], self.buf)

    def v(self, fn):
        return T(fn(self.ap), self.buf)


class Prog:
    def __init__(self, nc):
        self.nc = nc
        self.ops = {e: [] for e in ENGS}
        self.bufs = {}
        self.dmas = []

    def buf(self, name):
        b = self.bufs.get(name)
        if b is None:
            b = self.bufs[name] = Buf(name)
        return b

    def _bl(self, xs):
        out = []
        for x in xs:
            if x is None or isinstance(x, (int, float)):
                continue
            if isinstance(x, (list, tuple)):
                out.extend(self._bl(x))
            elif isinstance(x, T):
                out.append(self.buf(x.buf))
            elif isinstance(x, str):
                out.append(self.buf(x))
        return out

    def op(self, eng, fn, reads=(), writes=(), dma=None, extra=()):
        is_dma = dma is not None
        o = Op(eng, fn, is_dma, dma if is_dma else eng)
        R = self._bl(reads)
        W = self._bl(writes)
        deps = {}
        for b in R:
            if b.w is not None:
                deps[id(b.w)] = b.w
            if b.name.startswith("ps") and not is_dma:
                for r in b.r:
                    if r.eng != eng:
                        deps[id(r)] = r
        for b in W:
            if b.w is not None:
                deps[id(b.w)] = b.w
            for r in b.r:
                deps[id(r)] = r
        for d in extra:
            deps[id(d)] = d
        for d in deps.values():
            if d is o or d.fn is None:
                continue
            if (not d.is_dma) and (not is_dma) and d.eng == eng and fn is not None:
                if eng == "pe":
                    continue
                if not any(b.w is d for b in R):
                    continue
            o.deps.append(d)
            d.sig = True
        for b in R:
            b.r.append(o)
        for b in W:
            b.w = o
            b.r = []
        self.ops[eng].append(o)
        if is_dma:
            self.dmas.append(o)
        return o

    def barrier(self, dummies):
        dm = list(self.dmas)
        self.dmas = []
        self.op("act", lambda e: e.activation(out=dummies["act"], in_=dummies["act"], func=AF.Copy), writes=["bar_act"])
        self.op("dve", lambda e: e.memset(dummies["dve"], 0.0), writes=["bar_dve"])
        self.op("pool", lambda e: e.memset(dummies["pool"], 0.0), writes=["bar_pool"])
        for e in ENGS:
            self.op(e, None, reads=["bar_act", "bar_dve", "bar_pool"], extra=dm)

    def emit(self):
        nc = self.nc
        counts = {}
        keys = []
        for e in ENGS:
            for o in self.ops[e]:
                if o.fn is None:
                    continue
                if o.is_dma:
                    o.sig = True
                if o.sig:
                    k = o.semkey
                    if k not in counts:
                        counts[k] = 0
                        keys.append(k)
                    counts[k] += 16 if o.is_dma else 1
                    o.val = counts[k]
        final = list(self.dmas)
        with ExitStack() as st:
            sems = {k: st.enter_context(nc.semaphore("s_" + str(k))) for k in keys}
            block = st.enter_context(nc.Block())
            handles = {"pe": "tensor", "act": "scalar", "dve": "vector", "pool": "gpsimd", "sp": "sync"}

            def run(ename, eng):
                waited = {}

                def wait_for(dl):
                    need = {}
                    for d in dl:
                        if d.val > need.get(d.semkey, 0):
                            need[d.semkey] = d.val
                    for k, v in need.items():
                        if waited.get(k, 0) < v:
                            eng.wait_ge(sems[k], v)
                            waited[k] = v

                for o in self.ops[ename]:
                    wait_for(o.deps)
                    if o.fn is None:
                        continue
                    ins = o.fn(eng)
                    if o.sig:
                        ins.then_inc(sems[o.semkey], 16 if o.is_dma else 1)
                if ename == "sp":
                    wait_for(final)

            for ename in ENGS:
                def mk(ename=ename):
                    def _f(eng):
                        run(ename, eng)
                    return _f
                getattr(block, handles[ename])(mk())
        self.n_ops = {e: len(self.ops[e]) for e in ENGS}


class Arena:
    def __init__(self, t, nwords):
        self.t = t
        self.n = nwords
        self.off = 0

    def f32(self, n):
        assert self.off + n <= self.n, ("arena overflow", self.off, n, self.n)
        ap = self.t[:, self.off:self.off + n]
        self.off += n
        return ap

    def bf16(self, n):
        nw = (n + 1) // 2
        assert self.off + nw <= self.n, ("arena overflow", self.off, nw, self.n)
        ap = self.t[:, self.off:self.off + nw].bitcast(BF16)
        self.off += nw
        return ap


def r3(ap, a):
    return ap.rearrange("p (a b) -> p a b", a=a)


WSPEC = [
    ("w_q", 1024, 1024, 512), ("w_k", 1024, 256, None), ("w_v", 1024, 256, None), ("w_z", 1024, 2048, None), ("w_xbc", 1024, 3072, 512),
    ("w_dt", 1024, 64, None), ("w_g", 1024, 2048, 128), ("w_a", 1024, 1024, 128), ("w_s", 2048, 1024, 128), ("w_o", 1024, 1024, 512),
    ("w_up", 1024, 5632, 128), ("w_dn", 2816, 1024, 128), ("w_pl", 256, 1024, 128), ("w_pg", 1024, 1024, 128),
]
WMW = {n: mw for n, k, m, mw in WSPEC}


def wshape(k, m, mw):
    return [k, m] if mw is None else [(m // mw) * 128, (k // 128) * mw]


NCONST = 9
CV = {}
_cv_off = 0
for _n, _w in [("n1", 8), ("n2", 8), ("nf", 8), ("gb", 16), ("bpg", 8), ("qw", 1), ("kw", 1), ("cw0", 24), ("cw1", 24), ("cw2", 24),
               ("cb", 24), ("fw0", 44), ("fw1", 44), ("fw2", 44), ("fb", 44), ("dtb", 1), ("alog", 1)]:
    CV[_n] = (_cv_off, _w)
    _cv_off += _w
NCV = _cv_off


def build(S, debug=False):
    NT = S // 128
    TT = 512 if S >= 512 else S
    NTT = S // TT
    CPT = TT // 128
    nc = bass.Bass("TRN2", target_bir_lowering=False)
    P = Prog(nc)
    dt_in = lambda n, sh, d=F32: nc.dram_tensor(n, sh, d, kind="ExternalInput").ap()
    okind = "ExternalOutput" if debug else "Internal"
    dt_sc = lambda n, sh, d: nc.dram_tensor(n, sh, d, kind=okind).ap()
    x_d = dt_in("x", [S, D])
    p_d = dt_in("p", [S, 256])
    wsrc = {n: dt_in(n, wshape(k, m, mw)) for n, k, m, mw in WSPEC}
    consts_d = dt_in("consts", [128, NCONST, 128])
    cv_d = dt_in("cvec", [128, NCV])
    rows_d = dt_in("rows", [1, 1024 + 2048 + 32])
    cos_d = dt_in("cosT", [128, S])
    sin_d = dt_in("sinT", [128, S])
    y_d = nc.dram_tensor("y", [S, D], F32, kind="ExternalOutput").ap()
    wb = {n: dt_sc(n + "_b", wshape(k, m, mw), BF16) for n, k, m, mw in WSPEC}
    hT_d = dt_sc("hT_d", [128, 8, S + 2], BF16)
    xT_d = dt_sc("xT_d", [128, 8, S], F32)
    pT_d = dt_sc("pT_d", [128, 2, S], BF16)
    xtok_d = dt_sc("xtok_d", [S, 2048], BF16)
    btok_d = dt_sc("btok_d", [S, 512], BF16)
    bT_d = dt_sc("bT_d", [128, 4, S], BF16)
    cT_d = dt_sc("cT_d", [128, 4, S], BF16)
    dd_d = dt_sc("dd_d", [S, 128], F32)
    yf_d = dt_sc("yf_d", [S, 2048], F32)
    ssdT_d = dt_sc("ssdT_d", [128, 16, S], BF16)
    x1T_d = dt_sc("x1T_d", [128, 8, S], F32)
    h2T_d = dt_sc("h2T_d", [128, 8, S + 2], BF16)
    rs_d = nc.dram_tensor("rs_d", [4, 512], F32, kind="Internal").ap()

    with ExitStack() as st:
        NW_ALL = 51000
        sb_t = st.enter_context(nc.sbuf_tensor("sb", [128, NW_ALL], F32))
        ps_t = st.enter_context(nc.psum_tensor("psum", [128, 4096], F32))
        RES = Arena(sb_t, NW_ALL)
        PS = [T(ps_t[:, b * 512:(b + 1) * 512], "ps%d" % b) for b in range(8)]
        psbf = lambda b: T(ps_t[:, b * 512:(b + 1) * 512].bitcast(BF16), "ps%d" % b)
        ps2 = lambda b: T(ps_t[:, b * 512:(b + 2) * 512], "ps%d" % b)

        def bufs(*xs):
            return [x for x in xs if isinstance(x, (T, str))]

        def apof(x):
            return x.ap if isinstance(x, T) else x

        def mm(out, lhsT, rhs, start=True, stop=True, extra_w=()):
            P.op("pe", lambda e: e.matmul(out.ap, lhsT.ap, rhs.ap, start=start, stop=stop), reads=[lhsT, rhs], writes=[out, *extra_w])

        def tr(out, in_, ident, extra_w=()):
            P.op("pe", lambda e: e.transpose(out.ap, in_.ap, ident.ap), reads=[in_, ident], writes=[out, *extra_w])

        def act(out, in_, func, scale=None, bias=None, accum=None, extra_r=(), extra_w=()):
            kw = {}
            if scale is not None:
                kw["scale"] = apof(scale)
            if bias is not None:
                kw["bias"] = apof(bias)
            if accum is not None:
                kw["accum_out"] = accum.ap
            P.op("act", lambda e: e.activation(out=out.ap, in_=in_.ap, func=func, **kw),
                 reads=[in_, *bufs(scale, bias), *extra_r], writes=[out, *bufs(accum), *extra_w])

        def tt(eng, out, a, b, op, extra_r=(), extra_w=()):
            P.op(eng, lambda e: e.tensor_tensor(out=out.ap, in0=a.ap, in1=b.ap, op=op), reads=[a, b, *extra_r], writes=[out, *extra_w])

        def stt(eng, out, a, scalar, b, op0, op1, extra_r=()):
            P.op(eng, lambda e: e.scalar_tensor_tensor(out=out.ap, in0=a.ap, scalar=apof(scalar), in1=b.ap, op0=op0, op1=op1),
                 reads=[a, b, *bufs(scalar), *extra_r], writes=[out])

        def ts(eng, out, a, s1, s2, op0, op1=None, extra_r=()):
            if op1 is None:
                P.op(eng, lambda e: e.tensor_scalar(out=out.ap, in0=a.ap, scalar1=apof(s1), scalar2=None, op0=op0),
                     reads=[a, *bufs(s1), *extra_r], writes=[out])
            else:
                P.op(eng, lambda e: e.tensor_scalar(out=out.ap, in0=a.ap, scalar1=apof(s1), scalar2=apof(s2), op0=op0, op1=op1),
                     reads=[a, *bufs(s1, s2), *extra_r], writes=[out])

        def cp(eng, out, in_, extra_r=(), extra_w=()):
            if eng == "act":
                P.op("act", lambda e: e.copy(out=out.ap, in_=in_.ap), reads=[in_, *extra_r], writes=[out, *extra_w])
            else:
                P.op(eng, lambda e: e.tensor_copy(out=out.ap, in_=in_.ap), reads=[in_, *extra_r], writes=[out, *extra_w])

        def recip(out, in_):
            P.op("dve", lambda e: e.reciprocal(out=out.ap, in_=in_.ap), reads=[in_], writes=[out])

        def recip4(out, in_, n):
            w_ = n // 4
            for k_ in range(4):
                recip(out[:, k_ * w_:(k_ + 1) * w_], in_[:, k_ * w_:(k_ + 1) * w_])

        def memset(eng, out, val):
            P.op(eng, lambda e: e.memset(out.ap, val), writes=[out])

        def dma(out, in_, key, eng="sp", slow=False):
            return P.op(eng, lambda e: e.dma_start(out=apof(out), in_=apof(in_), allow_slow_non_contiguous=slow), reads=[in_], writes=[out], dma=key)

        cst_f = T(r3(RES.f32(NCONST * 128), NCONST), "cst_f")
        cst_b = T(r3(RES.bf16(NCONST * 128), NCONST), "cst_b")
        cvec = T(RES.f32(NCV), "cvec")
        acol = T(RES.f32(1), "acol")
        dum = {e: RES.f32(1) for e in ("act", "dve", "pool")}
        ident_f = cst_f[:, 0, :]
        ident_b, perm_b, oblk_b, ones_b, le_b, gt_b, ge_b, lt_b = [cst_b[:, i, :] for i in range(8)]
        sel_f = cst_f[:, 8, :]
        dma(cst_f, consts_d, "c0")
        dma(cvec, cv_d, "c1")
        cp("dve", cst_b, cst_f)
        cvs = lambda n, i=0: cvec[:, CV[n][0] + i:CV[n][0] + i + 1]
        act(acol[0:64, :], cvs("alog")[0:64, :], AF.Exp)
        ts("dve", acol[0:64, :], acol[0:64, :], -1.0, None, ALU.mult)
        const_mark = RES.off
        KT = [T(RES.bf16(S), "KT%d" % i) for i in range(2)]
        VA_flat = T(RES.bf16(NT * 4 * 65), "VA")
        VA = VA_flat.v(lambda a: a[:, 0:NT * 4 * 65].rearrange("p (c g e) -> p c g e", c=NT, g=4))
        res_mark = RES.off

        def phase_arena(early=False):
            a = Arena(sb_t, NW_ALL)
            a.off = const_mark if early else res_mark
            return a

        for n, k, m, mw in WSPEC:
            nr = wshape(k, m, mw)[0]
            rows = 512 if nr >= 512 else nr
            for r0 in range(0, nr, rows):
                r1 = min(nr, r0 + rows)
                dma(T(wb[n][r0:r1, :], "wb_" + n), T(wsrc[n][r0:r1, :], "wsrc_" + n), "wc", eng="pool")

        class WStream:
            def __init__(self, A, nslots=3):
                self.slots = [T(A.bf16(4096), "wslot%d" % i) for i in range(nslots)]
                self.ns = nslots
                self.specs = []
                self.idx = 0
                self.loaded = 0

            def plan(self, specs):
                self.specs.extend(specs)

            def _load(self, i):
                n, kc, m0, mw = self.specs[i]
                sl = self.slots[i % self.ns]
                assert WMW[n] == mw and m0 % mw == 0, (n, mw, m0)
                mt = m0 // mw
                src = T(wb[n][mt * 128:(mt + 1) * 128, :], "wb_" + n)
                dma(sl.v(lambda a: a[:, 0:kc * mw]), src, "w%d" % (i % self.ns))

            def next(self, name):
                while self.loaded < min(len(self.specs), self.idx + self.ns):
                    self._load(self.loaded)
                    self.loaded += 1
                n, kc, m0, mw = self.specs[self.idx]
                assert n == name, (n, name)
                sl = self.slots[self.idx % self.ns]
                self.idx += 1
                return sl.v(lambda a: r3(a[:, 0:kc * mw], kc))

        def rms_cols(A_sq, src_fn, nck, pb, out_rinv, n, scratch_sd):
            for m in range(nck):
                sq = A_sq[m % 2]
                tt("pool", sq, src_fn(m), src_fn(m), ALU.mult)
                mm(PS[pb][:, 0:n], ones_b, sq, start=(m == 0), stop=(m == nck - 1))
            act(scratch_sd, PS[pb][:, 0:n], AF.Sqrt, scale=1.0 / (128 * nck), bias=EPS)
            recip(out_rinv, scratch_sd)

        P.barrier(dum)
        A = phase_arena(True)
        w1bc = T(A.f32(1024), "w1bc")
        dma(w1bc, T(rows_d[0:1, 0:1024].partition_broadcast(128), "rows"), "c2")
        zt = T(A.bf16(16), "zt")
        memset("dve", zt, 0.0)
        for d_ in (hT_d, h2T_d):
            for col in (0, S + 1):
                dma(T(d_[:, :, col:col + 1], "halo"), zt.v(lambda a: r3(a[:, 0:8], 8)), "c3", slow=True)
        xt = [T(r3(A.f32(CPT * 1024), CPT), "A_xt%d" % i) for i in range(2)]
        pt = [T(r3(A.f32(CPT * 256), CPT), "A_pt%d" % i) for i in range(2)]
        hb = T(r3(A.bf16(CPT * 1024), CPT), "A_hb")
        pbf = T(r3(A.bf16(CPT * 256), CPT), "A_pbf")
        junk = T(A.f32(1024), "A_junk")
        ss = T(A.f32(4), "A_ss")
        sd = T(A.f32(4), "A_sd")
        rstd = T(A.f32(4), "A_rstd")
        hT_sb = T(r3(A.bf16(8 * TT), 8), "A_hT")
        xT_sb = T(r3(A.f32(8 * TT), 8), "A_xT")
        pT_sb = T(r3(A.bf16(2 * TT), 2), "A_pT")

        def loadA(j):
            dma(xt[j % 2], T(x_d[j * TT:(j + 1) * TT, :].rearrange("(c p) d -> p c d", p=128), "x"), "ax%d" % (j % 2))
            dma(pt[j % 2], T(p_d[j * TT:(j + 1) * TT, :].rearrange("(c p) d -> p c d", p=128), "p"), "ap%d" % (j % 2))

        loadA(0)
        for j in range(NTT):
            if j + 1 < NTT:
                loadA(j + 1)
            X = xt[j % 2]
            memset("dve", ss, 0.0)
            for c in range(CPT):
                act(junk, X[:, c, :], AF.Square, accum=ss[:, c:c + 1])
            act(sd[:, 0:CPT], ss[:, 0:CPT], AF.Sqrt, scale=1.0 / D, bias=EPS)
            recip(rstd[:, 0:CPT], sd[:, 0:CPT])
            for c in range(CPT):
                stt("dve", hb[:, c, :], X[:, c, :], rstd[:, c:c + 1], w1bc, ALU.mult, ALU.mult)
                cp("pool", pbf[:, c, :], pt[j % 2][:, c, :])
            for c in range(CPT):
                pb = c % 2
                for kc in range(8):
                    tr(psbf(pb)[:, kc * 128:(kc + 1) * 128], hb[:, c, kc * 128:(kc + 1) * 128], ident_b)
                cp("act", hT_sb[:, :, c * 128:(c + 1) * 128], psbf(pb).v(lambda a: r3(a, 8)))
                xb = 2 + 2 * (c % 2)
                for kc in range(8):
                    tr(PS[xb + kc // 4][:, (kc % 4) * 128:(kc % 4 + 1) * 128], X[:, c, kc * 128:(kc + 1) * 128], ident_f)
                for hh in range(2):
                    cp("dve", xT_sb[:, 4 * hh:4 * hh + 4, c * 128:(c + 1) * 128], PS[xb + hh].v(lambda a: r3(a, 4)))
                for kc in range(2):
                    tr(psbf(6 + c % 2)[:, kc * 128:(kc + 1) * 128], pbf[:, c, kc * 128:(kc + 1) * 128], ident_b)
                cp("act", pT_sb[:, :, c * 128:(c + 1) * 128], psbf(6 + c % 2)[:, 0:256].v(lambda a: r3(a, 2)))
            dma(T(hT_d[:, :, 1 + j * TT:1 + (j + 1) * TT], "hT_d%d" % j), hT_sb, "ao0")
            dma(T(xT_d[:, :, j * TT:(j + 1) * TT], "xT_d%d" % j), xT_sb, "ao1")
            dma(T(pT_d[:, :, j * TT:(j + 1) * TT], "pT_d%d" % j), pT_sb, "ao2")
        P.barrier(dum)

        def conv3(acc, main, halo, w0, w1, w2, b):
            n = TT
            act(acc, main, AF.Identity, scale=w1, bias=b)
            stt("dve", acc[:, 1:n], main[:, 0:n - 1], w0, acc[:, 1:n], ALU.mult, ALU.add)
            stt("dve", acc[:, 0:n - 1], main[:, 1:n], w2, acc[:, 0:n - 1], ALU.mult, ALU.add)
            stt("dve", acc[:, 0:1], halo[:, 0:1], w0, acc[:, 0:1], ALU.mult, ALU.add)
            stt("dve", acc[:, n - 1:n], halo[:, 1:2], w2, acc[:, n - 1:n], ALU.mult, ALU.add)

        def load_hTw(dst, src_d, j, key):
            dma(dst, T(src_d[:, :, j * TT:j * TT + TT + 2], "hTsrc"), key)

        A = phase_arena(True)
        WS = WStream(A)
        hTw = [T(r3(A.bf16(8 * (TT + 2)), 8), "B_hTw%d" % i) for i in range(2)]
        wdt = T(r3(A.bf16(8 * 64), 8), "B_wdt")
        dma(wdt, T(wb["w_dt"].rearrange("(kc p) m -> p kc m", p=128), "wb_w_dt"), "c4")
        xTc = T(r3(A.bf16(16 * TT), 16), "B_xTc")
        bTc = T(r3(A.bf16(4 * TT), 4), "B_bTc")
        cTc = T(r3(A.bf16(4 * TT), 4), "B_cTc")
        acc = [T(A.f32(TT), "B_acc%d" % i) for i in range(2)]
        ddT = T(A.f32(TT), "B_ddT")
        xtok = [T(A.bf16(2048), "B_xtok%d" % i) for i in range(2)]
        btok = [T(A.bf16(512), "B_btok%d" % i) for i in range(2)]
        ddtok = [T(A.f32(128), "B_ddtok%d" % i) for i in range(2)]
        for j in range(NTT):
            WS.plan([("w_xbc", 8, g * 512, 512) for g in range(6)])
        load_hTw(hTw[0], hT_d, 0, "bh0")
        for j in range(NTT):
            if j + 1 < NTT:
                load_hTw(hTw[(j + 1) % 2], hT_d, j + 1, "bh%d" % ((j + 1) % 2))
            H = hTw[j % 2]
            for g in range(6):
                wt = WS.next("w_xbc")
                for mi in range(4):
                    m = g * 4 + mi
                    pb = m % 2
                    for kc in range(8):
                        mm(PS[pb][:, 0:TT], wt[:, kc, mi * 128:(mi + 1) * 128], H[:, kc, 1:TT + 1], start=(kc == 0), stop=(kc == 7))
                    for kc in range(8):
                        mm(PS[2 + pb][:, 0:2], wt[:, kc, mi * 128:(mi + 1) * 128], H[:, kc, 0:TT + 2:TT + 1], start=(kc == 0), stop=(kc == 7))
                    ac = acc[m % 2]
                    conv3(ac, PS[pb][:, 0:TT], PS[2 + pb][:, 0:2], cvs("cw0", m), cvs("cw1", m), cvs("cw2", m), cvs("cb", m))
                    dst = xTc[:, m, :] if m < 16 else (bTc[:, m - 16, :] if m < 20 else cTc[:, m - 20, :])
                    act(dst, ac, AF.Silu)
            for kc in range(8):
                mm(PS[4][0:64, 0:TT], wdt[:, kc, :], H[:, kc, 1:TT + 1], start=(kc == 0), stop=(kc == 7))
            act(ddT[0:64, :], PS[4][0:64, 0:TT], AF.Exp, bias=cvs("dtb")[0:64, :])
            act(ddT[0:64, :], ddT[0:64, :], AF.Ln, bias=1.0)
            ts("dve", ddT[64:128, :], ddT[0:64, :], acol[0:64, :], None, ALU.mult)
            dma(T(bT_d[:, :, j * TT:(j + 1) * TT], "bT_d%d" % j), bTc, "bo0")
            dma(T(cT_d[:, :, j * TT:(j + 1) * TT], "cT_d%d" % j), cTc, "bo1")
            for c in range(CPT):
                cg = j * CPT + c
                q = c % 2
                for m in range(16):
                    tr(psbf(5 + m // 8)[:, (m % 8) * 128:(m % 8 + 1) * 128], xTc[:, m, c * 128:(c + 1) * 128], ident_b)
                cp("act", xtok[q][:, 0:1024], psbf(5))
                cp("dve", xtok[q][:, 1024:2048], psbf(6))
                for m in range(4):
                    tr(psbf(7)[:, m * 128:(m + 1) * 128], bTc[:, m, c * 128:(c + 1) * 128], ident_b)
                cp("act", btok[q], psbf(7)[:, 0:512])
                tr(PS[4][:, 0:128], ddT[:, c * 128:(c + 1) * 128], ident_f)
                cp("dve", ddtok[q], PS[4][:, 0:128])
                dma(T(xtok_d[cg * 128:(cg + 1) * 128, :], "xtok_d%d" % cg), xtok[q], "bo2%d" % q)
                dma(T(btok_d[cg * 128:(cg + 1) * 128, :], "btok_d%d" % cg), btok[q], "bo3%d" % q)
                dma(T(dd_d[cg * 128:(cg + 1) * 128, :], "dd_d%d" % cg), ddtok[q], "bo4%d" % q)
        P.barrier(dum)

        A = phase_arena(True)
        s_x = [T(A.bf16(2048), "S_x%d" % i) for i in range(2)]
        s_b = [T(A.bf16(512), "S_b%d" % i) for i in range(2)]
        s_bT = [T(r3(A.bf16(512), 4), "S_bT%d" % i) for i in range(2)]
        s_cT = [T(r3(A.bf16(512), 4), "S_cT%d" % i) for i in range(2)]
        s_dd = [T(A.f32(128), "S_dd%d" % i) for i in range(2)]
        s_adtb = T(A.bf16(32), "S_adtb")
        s_E = T(A.f32(96), "S_E")
        s_xdt = T(A.bf16(2048), "S_xdt")
        s_xdtA = T(A.bf16(2048), "S_xdtA")
        s_at = [T(A.bf16(512), "S_at%d" % i) for i in range(3)]
        s_dec = [T(A.bf16(512), "S_dec%d" % i) for i in range(3)]
        s_M = [T(A.bf16(512), "S_M%d" % i) for i in range(3)]
        s_cbm = T(A.bf16(512), "S_cbm")
        s_h = T(A.f32(2048), "S_h")
        s_hb = T(A.bf16(2048), "S_hb")
        s_tmp = T(A.f32(512), "S_tmp")
        s_y = [T(A.f32(2048), "S_y%d" % i) for i in range(2)]
        s_wz = T(r3(A.bf16(8 * 2048), 8), "S_wz")
        s_hT = [T(r3(A.bf16(8 * 128), 8), "S_hT%d" % i) for i in range(3)]
        s_yf = [T(A.f32(2048), "S_yf%d" % i) for i in range(2)]
        s_nw = T(A.f32(2048), "S_nw")
        s_dk = T(A.f32(32), "S_dk")
        s_sz = [T(A.f32(512), "S_sz%d" % i) for i in range(2)]
        s_zs = [T(A.f32(512), "S_zs%d" % i) for i in range(2)]
        s_gs = T(A.f32(4), "S_gs")
        s_gsd = T(A.f32(4), "S_gsd")
        s_gr = T(A.f32(4), "S_gr")
        s_junk = T(A.f32(512), "S_junk")
        s_sb = T(A.bf16(2048), "S_sb")
        s_sT = T(r3(A.bf16(16 * 128), 16), "S_sT")
        dma(s_wz, T(wb["w_z"].rearrange("(kc p) m -> p kc m", p=128), "wb_w_z"), "c5")
        dma(s_nw, T(rows_d[0:1, 1024:3072].partition_broadcast(128), "rows"), "c6")
        dma(s_dk, T(rows_d[0:1, 3072:3104].partition_broadcast(128), "rows"), "c7")
        bc_h = lambda t, n: t.v(lambda a: a.unsqueeze(2).to_broadcast([128, n, 64]))
        s_did = T(r3(A.bf16(32 * 128), 32), "S_did")
        for h in range(32):
            ts("dve", s_did[:, h, :], ident_b, s_dk[:, h:h + 1], None, ALU.mult)

        def ssd_load(c, q, di, t3=0):
            dma(s_x[q], T(xtok_d[c * 128:(c + 1) * 128, :], "xtok_d%d" % c), "sl0%d" % q)
            dma(s_b[q], T(btok_d[c * 128:(c + 1) * 128, :], "btok_d%d" % c), "sl1%d" % q)
            dma(s_bT[q], T(bT_d[:, :, c * 128:(c + 1) * 128], "bT_d%d" % (c // CPT)), "sl2%d" % q)
            dma(s_cT[q], T(cT_d[:, :, c * 128:(c + 1) * 128], "cT_d%d" % (c // CPT)), "sl3%d" % q)
            dma(s_dd[q], T(dd_d[c * 128:(c + 1) * 128, :], "dd_d%d" % c), "sl4%d" % q)
            if di == 1:
                dma(s_yf[q], T(yf_d[c * 128:(c + 1) * 128, :], "yf_d%d" % c), "sl5%d" % q)
                dma(s_hT[t3], T(hT_d[:, :, 1 + c * 128:1 + (c + 1) * 128], "hT_d%d" % (c // CPT)), "sl6%d" % t3)

        for di in range(2):
            tri, U, mask = (le_b, gt_b, le_b) if di == 0 else (ge_b, lt_b, ge_b)
            order = list(range(NT)) if di == 0 else list(range(NT - 1, -1, -1))
            memset("dve", s_h, 0.0)
            memset("pool", s_hb, 0.0)
            ssd_load(order[0], 0, di, 0)
            pend_tail = []
            for it, c in enumerate(order):
                q = it % 2
                if it + 1 < NT:
                    ssd_load(order[it + 1], (it + 1) % 2, di, (it + 1) % 3)
                dtc = s_dd[q][:, di * 32:(di + 1) * 32]
                adt = s_dd[q][:, 64 + di * 32:64 + (di + 1) * 32]
                cp("pool", s_adtb, adt)
                mm(PS[0][:, 0:32], tri, s_adtb)
                mm(PS[0][:, 32:64], U, s_adtb)
                mm(PS[0][:, 64:96], ones_b, s_adtb)
                act(s_E, PS[0][:, 0:96], AF.Exp)
                eac, dA, cd = s_E[:, 0:32], s_E[:, 32:64], s_E[:, 64:96]
                X3 = s_x[q].v(lambda a: r3(a, 32))
                for g in range(4):
                    mm(PS[1][:, g * 128:(g + 1) * 128], s_bT[q][:, g, :], s_cT[q][:, g, :])
                tt("dve", s_cbm.v(lambda a: r3(a, 4)), PS[1].v(lambda a: r3(a, 4)),
                   mask.v(lambda a: a.unsqueeze(1).to_broadcast([128, 4, 128])), ALU.mult)
                Y = s_y[q]
                hgs = [(g, hh) for g in range(4) for hh in range(2)]

                SEGB = (2, 3, 7)

                def stage1(i):
                    g, hh = hgs[i]
                    k = i % 3
                    h0 = g * 8 + hh * 4
                    tt("pool" if i % 2 == 0 else "dve", s_at[k].v(lambda a: r3(a, 4)),
                       tri.v(lambda a: a.unsqueeze(1).to_broadcast([128, 4, 128])),
                       adt[:, h0:h0 + 4].v(lambda a: a.unsqueeze(2).to_broadcast([128, 4, 128])), ALU.mult)
                    mm(PS[SEGB[k]], U, s_at[k])
                    act(s_dec[k], PS[SEGB[k]], AF.Exp)
                    tt("dve", s_M[k].v(lambda a: r3(a, 4)), s_dec[k].v(lambda a: r3(a, 4)),
                       s_cbm[:, g * 128:(g + 1) * 128].v(lambda a: a.unsqueeze(1).to_broadcast([128, 4, 128])), ALU.mult)

                def stage2(i):
                    g, hh = hgs[i]
                    k = i % 3
                    h0 = g * 8 + hh * 4
                    for hi in range(4):
                        h = h0 + hi
                        oy = PS[4 + g % 2][:, (hh * 4 + hi) * 64:(hh * 4 + hi + 1) * 64]
                        mm(oy, s_M[k][:, hi * 128:(hi + 1) * 128], s_xdt[:, h * 64:(h + 1) * 64], start=True, stop=(di == 0))
                        if di == 1:
                            mm(oy, s_did[:, h, :], s_x[q][:, h * 64:(h + 1) * 64], start=False, stop=True)
                    if hh == 1:
                        mm(PS[6], s_cT[q][:, g, :], s_hb[:, g * 512:(g + 1) * 512])
                        tt("dve", s_tmp.v(lambda a: r3(a, 8)), PS[6].v(lambda a: r3(a, 8)), bc_h(eac[:, g * 8:(g + 1) * 8], 8), ALU.mult)
                        if di == 1:
                            tt("dve", s_tmp, s_tmp, s_yf[q][:, g * 512:(g + 1) * 512], ALU.add)
                        tt("dve", Y[:, g * 512:(g + 1) * 512], PS[4 + g % 2], s_tmp, ALU.add)

                stage1(0)
                stage1(1)
                for g_ in range(4):
                    tt("pool", s_xdt[:, g_ * 512:(g_ + 1) * 512].v(lambda a: r3(a, 8)), s_x[q][:, g_ * 512:(g_ + 1) * 512].v(lambda a: r3(a, 8)),
                       bc_h(dtc[:, g_ * 8:(g_ + 1) * 8], 8), ALU.mult)
                for i in range(8):
                    if i + 2 < 8:
                        stage1(i + 2)
                    stage2(i)
                    if pend_tail:
                        for f_ in pend_tail.pop(0):
                            f_()
                    if i in (1, 2, 3, 4):
                        g_ = i - 1
                        tt("pool", s_xdtA[:, g_ * 512:(g_ + 1) * 512].v(lambda a: r3(a, 8)), s_xdt[:, g_ * 512:(g_ + 1) * 512].v(lambda a: r3(a, 8)),
                           bc_h(dA[:, g_ * 8:(g_ + 1) * 8], 8), ALU.mult)
                    if i in (4, 5, 6, 7):
                        g_ = i - 4
                        tt("pool", s_h[:, g_ * 512:(g_ + 1) * 512].v(lambda a: r3(a, 8)), s_h[:, g_ * 512:(g_ + 1) * 512].v(lambda a: r3(a, 8)),
                           bc_h(cd[:, g_ * 8:(g_ + 1) * 8], 8), ALU.mult)
                for g in range(4):
                    mm(PS[g % 2], s_b[q][:, g * 128:(g + 1) * 128], s_xdtA[:, g * 512:(g + 1) * 512])
                    tt("dve", s_h[:, g * 512:(g + 1) * 512], PS[g % 2], s_h[:, g * 512:(g + 1) * 512], ALU.add)
                cp("act", s_hb, s_h)
                if di == 0:
                    dma(T(yf_d[c * 128:(c + 1) * 128, :], "yf_d%d" % c), Y, "so0%d" % q)
                else:
                    while pend_tail:
                        for f_ in pend_tail.pop(0):
                            f_()
                    def make_tail(c=c, Y=Y, hT=s_hT[it % 3]):
                        def z_mm(g):
                            for kc in range(8):
                                mm(PS[g % 2], hT[:, kc, :], s_wz[:, kc, g * 512:(g + 1) * 512], start=(kc == 0), stop=(kc == 7))
                        def f_silu(g):
                            act(s_sz[g % 2], PS[g % 2], AF.Tanh, scale=0.5)
                            act(s_zs[g % 2], PS[g % 2], AF.Identity, scale=0.5)

                        def f_mul(g):
                            Yg = Y[:, g * 512:(g + 1) * 512]
                            stt("dve", Yg, s_sz[g % 2], 1.0, Yg, ALU.add, ALU.mult)
                            tt("pool", Yg, Yg, s_zs[g % 2], ALU.mult)
                        f_sq = lambda g: act(s_junk, Y[:, g * 512:(g + 1) * 512], AF.Square, accum=s_gs[:, g:g + 1])
                        slots = []
                        for t in range(7):
                            sl = []
                            if t == 0:
                                sl.append(lambda: memset("dve", s_gs, 0.0))
                            for fn_, off in ((z_mm, 0), (f_silu, 1), (f_mul, 2), (f_sq, 3)):
                                g = t - off
                                if 0 <= g < 4:
                                    sl.append(lambda fn_=fn_, g=g: fn_(g))
                            slots.append(sl)
                        slots[6].append(lambda: act(s_gsd, s_gs, AF.Sqrt, scale=1.0 / 512, bias=EPS))

                        def fin_a():
                            recip(s_gr, s_gsd)
                            for g in range(4):
                                stt("dve", s_sb[:, g * 512:(g + 1) * 512], Y[:, g * 512:(g + 1) * 512], s_gr[:, g:g + 1],
                                    s_nw[:, g * 512:(g + 1) * 512], ALU.mult, ALU.mult)

                        def fin_b():
                            for m in range(16):
                                tr(psbf(0 if m < 8 else 1)[:, (m % 8) * 128:(m % 8 + 1) * 128], s_sb[:, m * 128:(m + 1) * 128], ident_b)
                            cp("act", s_sT[:, 0:8, :], psbf(0).v(lambda a: r3(a, 8)))
                            cp("dve", s_sT[:, 8:16, :], psbf(1).v(lambda a: r3(a, 8)))
                            dma(T(ssdT_d[:, :, c * 128:(c + 1) * 128], "ssdT_d%d" % (c // CPT)), s_sT, "so1")
                        slots.append([fin_a])
                        slots.append([fin_b])
                        return slots
                    pend_tail = make_tail()
            while pend_tail:
                for f_ in pend_tail.pop(0):
                    f_()
            P.barrier(dum)

        A = phase_arena()
        hTw = [T(r3(A.bf16(8 * (TT + 2)), 8), "K_hTw%d" % i) for i in range(2)]
        wk = T(r3(A.bf16(8 * 256), 8), "K_wk")
        wv = T(r3(A.bf16(8 * 256), 8), "K_wv")
        dma(wk, T(wb["w_k"].rearrange("(kc p) m -> p kc m", p=128), "wb_w_k"), "c8")
        dma(wv, T(wb["w_v"].rearrange("(kc p) m -> p kc m", p=128), "wb_w_v"), "c9")
        cs = [[T(A.f32(TT), "K_cs%d%d" % (i, k)) for k in range(2)] for i in range(2)]
        memset("dve", VA_flat, 1.0)

        class Rope:
            def __init__(self, A, pfx):
                self.sq = T(A.bf16(TT), pfx + "sq")
                self.sd = T(A.f32(TT), pfx + "sd")
                self.ri = T(A.f32(TT), pfx + "ri")
                self.kn = T(A.bf16(TT), pfx + "kn")
                self.t1 = T(A.f32(TT), pfx + "t1")
                self.t2 = T(A.f32(TT), pfx + "t2")

            def run(self, src_ps, wcol, cosT, sinT, dst, pbs, pbs2):
                act(self.sq, src_ps, AF.Square)
                act(self.kn, src_ps, AF.Identity, scale=wcol)
                mm(PS[pbs][:, 0:TT], oblk_b, self.sq)
                mm(PS[pbs2][:, 0:TT], perm_b, self.kn)
                act(self.sd, PS[pbs][:, 0:TT], AF.Sqrt, scale=1.0 / 64, bias=EPS)
                tt("pool", self.t1, self.kn, cosT, ALU.mult)
                tt("dve", self.t2, PS[pbs2][:, 0:TT], sinT, ALU.mult)
                recip4(self.ri, self.sd, TT)
                tt("pool", self.t1, self.t1, self.t2, ALU.add)
                tt("dve", dst, self.t1, self.ri, ALU.mult)

        RP = Rope(A, "K_")

        def load_cs(dst, j, key):
            dma(dst[0], T(cos_d[:, j * TT:(j + 1) * TT], "cos"), key + "a")
            dma(dst[1], T(sin_d[:, j * TT:(j + 1) * TT], "sin"), key + "b")

        load_hTw(hTw[0], hT_d, 0, "kh0")
        load_cs(cs[0], 0, "kc0")
        for j in range(NTT):
            if j + 1 < NTT:
                load_hTw(hTw[(j + 1) % 2], hT_d, j + 1, "kh%d" % ((j + 1) % 2))
                load_cs(cs[(j + 1) % 2], j + 1, "kc%d" % ((j + 1) % 2))
            H = hTw[j % 2]
            for pp in range(2):
                for kc in range(8):
                    mm(PS[pp][:, 0:TT], wk[:, kc, pp * 128:(pp + 1) * 128], H[:, kc, 1:TT + 1], start=(kc == 0), stop=(kc == 7))
                RP.run(PS[pp][:, 0:TT], cvs("kw"), cs[j % 2][0], cs[j % 2][1], KT[pp][:, j * TT:(j + 1) * TT], 2 + pp, 4 + pp)
            for c in range(CPT):
                cg = j * CPT + c
                pb = 6 + c % 2
                for kc in range(8):
                    mm(PS[pb][:, 0:256], H[:, kc, 1 + c * 128:1 + (c + 1) * 128], wv[:, kc, :], start=(kc == 0), stop=(kc == 7))
                cp("act" if c % 2 == 0 else "dve", VA[:, cg, :, 0:64], PS[pb][:, 0:256].v(lambda a: r3(a, 4)))
        P.barrier(dum)

        A = phase_arena()
        WS = WStream(A)
        hT1 = T(r3(A.bf16(8 * TT), 8), "E_hT")
        cs1 = [T(A.f32(TT), "E_cs%d" % k) for k in range(2)]
        QT = T(r3(A.bf16(8 * TT), 8), "E_QT")
        attnT = T(r3(A.bf16(8 * TT), 8), "E_attnT")
        PT = [T(A.bf16(2 * TT), "E_PT%d" % i) for i in range(2)]
        ssdT = T(r3(A.bf16(16 * TT), 16), "E_ssdT")
        xT = T(r3(A.f32(8 * TT), 8), "E_xT")
        RQ = Rope(A, "E_")
        RQ2 = [RQ, Rope(A, "E2_")]
        o_sb = T(A.f32(TT), "E_osb")
        o_rec = T(A.f32(TT), "E_orec")
        ga, gs_, m1, m2 = RQ.sd, RQ.ri, RQ.t1, RQ.t2
        sqb = [T(A.bf16(TT), "E_sqb%d" % i) for i in range(2)]
        rinv, sdt = o_sb, o_rec
        scale = 64 ** -0.5
        for j in range(NTT):
            WS.plan([("w_q", 8, g * 512, 512) for g in range(2)])
            for m in range(8):
                WS.plan([("w_a", 8, m * 128, 128), ("w_s", 16, m * 128, 128), ("w_g", 8, m * 128, 128), ("w_g", 8, 1024 + m * 128, 128)])
            WS.plan([("w_o", 8, g * 512, 512) for g in range(2)])
        def loadE_early(j):
            dma(hT1, T(hT_d[:, :, 1 + j * TT:1 + (j + 1) * TT], "hT_d%d" % j), "eh")
            load_cs(cs1, j, "ec")
            dma(ssdT, T(ssdT_d[:, :, j * TT:(j + 1) * TT], "ssdT_d%d" % j), "es")

        loadE_early(0)
        for j in range(NTT):
            dma(xT, T(xT_d[:, :, j * TT:(j + 1) * TT], "xT_d%d" % j), "ex")
            for g in range(2):
                wt = WS.next("w_q")
                for mi in range(4):
                    m = g * 4 + mi
                    pb = m % 2
                    for kc in range(8):
                        mm(PS[pb][:, 0:TT], wt[:, kc, mi * 128:(mi + 1) * 128], hT1[:, kc, :], start=(kc == 0), stop=(kc == 7))
                    RQ2[m % 2].run(PS[pb][:, 0:TT], cvs("qw"), cs1[0], cs1[1], QT[:, m, :], 2 + pb, 4 + pb)
            iters = [(pp, r, kb) for pp in range(2) for r in range(4) for kb in range(NT)]
            bcs = [[o_sb, RQ.t1], [RQ2[1].t1, RQ2[1].t2]]
            orec = [o_rec, RQ.t2]
            npair = [0]

            def emit_qk(i):
                pp, r, kb = iters[i]
                jq = pp * 4 + r
                sb_ = (i % 2) * 2
                pt_ = PT[i % 2]
                mm(PS[sb_][:, 0:TT], KT[pp][0:64, kb * 128:(kb + 1) * 128], QT[0:64, jq, :])
                mm(PS[sb_ + 1][:, 0:TT], KT[pp][64:128, kb * 128:(kb + 1) * 128], QT[64:128, jq, :])
                P.op("act", lambda e, o=pt_.ap, i_=ps_t[:, sb_ * 512:sb_ * 512 + 2 * 512]: e.activation(out=o, in_=i_, func=AF.Exp, scale=scale),
                     reads=[PS[sb_], PS[sb_ + 1]], writes=[pt_])

            emit_qk(0)
            for i in range(len(iters)):
                if i + 1 < len(iters):
                    emit_qk(i + 1)
                pp, r, kb = iters[i]
                pt_ = PT[i % 2]
                par = (i // NT) % 2
                ob = 4 + 2 * par
                mm(PS[ob][0:65, 0:TT], VA[:, kb, 2 * pp, :], pt_[:, 0:TT], start=(kb == 0), stop=(kb == NT - 1))
                mm(PS[ob + 1][0:65, 0:TT], VA[:, kb, 2 * pp + 1, :], pt_[:, TT:2 * TT], start=(kb == 0), stop=(kb == NT - 1))
                if kb == NT - 1:
                    jq = pp * 4 + r
                    for half in range(2):
                        rr = orec[par][64:65, :] if half == 0 else orec[par][32:33, :]
                        recip(rr, PS[ob + half][64:65, 0:TT])
                        rsd = T(rs_d[2 * par + half:2 * par + half + 1, 0:TT], "rs_d%d%d" % (par, half))
                        dma(rsd, rr, "rs%d%d" % (par, half))
                        dma(bcs[par][half][0:64, :], rsd.v(lambda a: a.partition_broadcast(64)), "rb%d%d" % (par, half))
                    for half in range(2):
                        tt("dve", attnT[half * 64:(half + 1) * 64, jq, :], PS[ob + half][0:64, 0:TT], bcs[par][half][0:64, :], ALU.mult)
            MT = QT
            for m in range(8):
                wa = WS.next("w_a")
                for kc in range(8):
                    mm(PS[0][:, 0:TT], wa[:, kc, :], attnT[:, kc, :], start=(kc == 0), stop=(kc == 7))
                wsd = WS.next("w_s")
                for kc in range(16):
                    mm(PS[1][:, 0:TT], wsd[:, kc, :], ssdT[:, kc, :], start=(kc == 0), stop=(kc == 15))
                wg1 = WS.next("w_g")
                for kc in range(8):
                    mm(PS[2][:, 0:TT], wg1[:, kc, :], hT1[:, kc, :], start=(kc == 0), stop=(kc == 7))
                wg2 = WS.next("w_g")
                for kc in range(8):
                    mm(PS[3][:, 0:TT], wg2[:, kc, :], hT1[:, kc, :], start=(kc == 0), stop=(kc == 7))
                act(ga, PS[2][:, 0:TT], AF.Sigmoid, bias=cvs("gb", m))
                act(gs_, PS[3][:, 0:TT], AF.Sigmoid, bias=cvs("gb", 8 + m))
                tt("dve", m1, PS[0][:, 0:TT], ga, ALU.mult)
                tt("dve", m2, PS[1][:, 0:TT], gs_, ALU.mult)
                tt("pool", MT[:, m, :], m1, m2, ALU.add)
            if j + 1 < NTT:
                loadE_early(j + 1)
            for g in range(2):
                wo = WS.next("w_o")
                for mi in range(4):
                    m = g * 4 + mi
                    pb = m % 2
                    for kc in range(8):
                        mm(PS[pb][:, 0:TT], wo[:, kc, mi * 128:(mi + 1) * 128], MT[:, kc, :], start=(kc == 0), stop=(kc == 7))
                    tt("dve", xT[:, m, :], PS[pb][:, 0:TT], xT[:, m, :], ALU.add)
            dma(T(x1T_d[:, :, j * TT:(j + 1) * TT], "x1T_d%d" % j), xT, "eo0")
            rms_cols(sqb, lambda m: xT[:, m, :], 8, 7, rinv, TT, sdt)
            H2 = attnT
            for m in range(8):
                stt("dve", H2[:, m, :], xT[:, m, :], cvs("n2", m), rinv, ALU.mult, ALU.mult)
            dma(T(h2T_d[:, :, 1 + j * TT:1 + (j + 1) * TT], "h2T_d%d" % j), H2, "eo1")
        P.barrier(dum)

        A = phase_arena()
        WS = WStream(A)
        h2w = T(r3(A.bf16(8 * (TT + 2)), 8), "F_h2w")
        x1 = T(r3(A.f32(8 * TT), 8), "F_x1")
        pT = T(r3(A.bf16(2 * TT), 2), "F_pT")
        _g = A.f32(11 * TT)
        gT = T(r3(_g.bitcast(BF16), 22), "F_gT")
        yT = T(r3(_g[:, 0:8 * TT], 8), "F_gT")
        accg = T(A.f32(TT), "F_accg")
        accv = T(A.f32(TT), "F_accv")
        gl = T(A.f32(TT), "F_gl")
        x2b = T(r3(A.bf16(8 * TT), 8), "F_x2b")
        sg = T(A.f32(TT), "F_sg")
        sqb = [T(A.bf16(TT), "F_sqb%d" % i) for i in range(2)]
        rinv = T(A.f32(TT), "F_rinv")
        sdt = T(A.f32(TT), "F_sdt")
        yo = [T(A.f32(1024), "F_yo%d" % i) for i in range(2)]
        for j in range(NTT):
            for m in range(22):
                WS.plan([("w_up", 8, m * 128, 128), ("w_up", 8, 2816 + m * 128, 128)])
            WS.plan([("w_dn", 22, m * 128, 128) for m in range(8)])
            for m in range(8):
                WS.plan([("w_pg", 8, m * 128, 128), ("w_pl", 2, m * 128, 128)])
        load_hTw(h2w, h2T_d, 0, "fh")
        dma(pT, T(pT_d[:, :, 0:TT], "pT_d0"), "fp")
        for j in range(NTT):
            dma(x1, T(x1T_d[:, :, j * TT:(j + 1) * TT], "x1T_d%d" % j), "fx")
            for m in range(22):
                for half, accx in ((0, accg), (1, accv)):
                    wt = WS.next("w_up")
                    mc = m + 22 * half
                    pb = half
                    for kc in range(8):
                        mm(PS[pb][:, 0:TT], wt[:, kc, :], h2w[:, kc, 1:TT + 1], start=(kc == 0), stop=(kc == 7))
                    for kc in range(8):
                        mm(PS[2 + pb][:, 0:2], wt[:, kc, :], h2w[:, kc, 0:TT + 2:TT + 1], start=(kc == 0), stop=(kc == 7))
                    conv3(accx, PS[pb][:, 0:TT], PS[2 + pb][:, 0:2], cvs("fw0", mc), cvs("fw1", mc), cvs("fw2", mc), cvs("fb", mc))
                act(gl, accg, AF.Gelu_apprx_tanh)
                tt("pool", gT[:, m, :], gl, accv, ALU.mult)
            if j + 1 < NTT:
                load_hTw(h2w, h2T_d, j + 1, "fh")
            for m in range(8):
                wt = WS.next("w_dn")
                for kc in range(22):
                    mm(PS[4 + m % 2][:, 0:TT], wt[:, kc, :], gT[:, kc, :], start=(kc == 0), stop=(kc == 21))
                tt("dve", x1[:, m, :], PS[4 + m % 2][:, 0:TT], x1[:, m, :], ALU.add)
                cp("pool", x2b[:, m, :], x1[:, m, :])
            for m in range(8):
                wt = WS.next("w_pg")
                for kc in range(8):
                    mm(PS[6][:, 0:TT], wt[:, kc, :], x2b[:, kc, :], start=(kc == 0), stop=(kc == 7))
                wt = WS.next("w_pl")
                for kc in range(2):
                    mm(PS[7][:, 0:TT], wt[:, kc, :], pT[:, kc, :], start=(kc == 0), stop=(kc == 1))
                act(sg, PS[6][:, 0:TT], AF.Sigmoid, bias=cvs("bpg", m))
                tt("dve", sg, PS[7][:, 0:TT], sg, ALU.mult)
                tt("pool", x1[:, m, :], x1[:, m, :], sg, ALU.add)
            if j + 1 < NTT:
                dma(pT, T(pT_d[:, :, (j + 1) * TT:(j + 2) * TT], "pT_d%d" % (j + 1)), "fp")
            rms_cols(sqb, lambda m: x1[:, m, :], 8, 0, rinv, TT, sdt)
            for m in range(8):
                stt("dve", yT[:, m, :], x1[:, m, :], cvs("nf", m), rinv, ALU.mult, ALU.mult)
            for c in range(CPT):
                cg = j * CPT + c
                q = c % 2
                b0 = 1 + 2 * q
                for m in range(8):
                    tr(PS[b0 + m // 4][:, (m % 4) * 128:(m % 4 + 1) * 128], yT[:, m, c * 128:(c + 1) * 128], ident_f)
                cp("act", yo[q][:, 0:512], PS[b0])
                cp("dve", yo[q][:, 512:1024], PS[b0 + 1])
                dma(T(y_d[cg * 128:(cg + 1) * 128, :], "y_d%d" % cg), yo[q], "fo%d" % q)
        P.emit()
    return nc, P


def _consts():
    c = np.zeros((128, NCONST, 128), np.float32)
    i = np.arange(128)
    c[:, 0, :] = np.eye(128)
    pm = np.zeros((128, 128), np.float32)
    pm[i, i ^ 1] = 1.0
    c[:, 1, :] = pm
    c[:, 2, :] = (i[:, None] // 64 == i[None, :] // 64)
    c[:, 3, :] = 1.0
    c[:, 4, :] = i[:, None] <= i[None, :]
    c[:, 5, :] = i[:, None] > i[None, :]
    c[:, 6, :] = i[:, None] >= i[None, :]
    c[:, 7, :] = i[:, None] < i[None, :]
    c[64, 8, :] = 1.0
    return c


def _rope_tables(S):
    rows = S // 64
    row_idx = np.repeat(np.arange(rows, dtype=np.float32), 64)
    col_idx = np.tile(np.arange(64, dtype=np.float32), rows)
    inv_freq = (np.float32(10000.0) ** (-np.arange(0, 32, 2, dtype=np.float32) / np.float32(32))).astype(np.float32)
    ang = np.concatenate([row_idx[:, None] * inv_freq, col_idx[:, None] * inv_freq], axis=-1).astype(np.float32)
    cos, sin = np.cos(ang), np.sin(ang)
    pidx = (np.arange(128) % 64) // 2
    sign = np.where(np.arange(128) % 2 == 0, -1.0, 1.0).astype(np.float32)
    cosT = np.ascontiguousarray(cos[:, pidx].T).astype(np.float32)
    sinT = np.ascontiguousarray((sin[:, pidx] * sign[None, :]).T).astype(np.float32)
    return cosT, sinT


def _pack_weights(w):
    win = w["w_in"][0]
    o = {}
    q = win[:, 0:1024]
    qperm = []
    for pp in range(2):
        for r in range(4):
            for half in range(2):
                hq = (2 * pp + half) * 4 + r
                qperm.extend(range(hq * 64, hq * 64 + 64))
    qperm = np.array(qperm)
    o["w_q"] = q[:, qperm]
    o["w_k"] = win[:, 1024:1280]
    o["w_v"] = win[:, 1280:1536]
    o["w_z"] = win[:, 1536:3584]
    o["w_xbc"] = win[:, 3584:6656]
    o["w_dt"] = win[:, 6656:6720]
    o["w_g"] = win[:, 6720:8768]
    o["w_a"] = w["w_attn_branch"][0][qperm, :]
    o["w_s"] = w["w_ssd_branch"][0]
    o["w_o"] = w["w_out"][0]
    o["w_up"] = w["w_up"][0]
    o["w_dn"] = w["w_down"][0]
    o["w_pl"] = w["w_ple"][0]
    o["w_pg"] = w["w_ple_gate"][0]
    for n, k, m, mw in WSPEC:
        if mw is not None:
            o[n] = o[n].reshape(k // 128, 128, m // mw, mw).transpose(2, 1, 0, 3).reshape((m // mw) * 128, (k // 128) * mw)
    o = {k: np.ascontiguousarray(v, dtype=np.float32) for k, v in o.items()}
    cv = np.zeros((128, NCV), np.float32)

    def put(name, vec):
        off, wd = CV[name]
        v = np.asarray(vec, np.float32).reshape(-1)
        if v.size == 64 * wd and wd == 1:
            if name in ("qw", "kw"):
                cv[:, off] = np.concatenate([v, v])
            else:
                cv[0:64, off] = v
        else:
            cv[:, off:off + wd] = v.reshape(wd, 128).T

    put("n1", w["norm1_w"][0]); put("n2", w["norm2_w"][0]); put("nf", w["final_norm_w"])
    put("gb", w["gate_b"][0]); put("bpg", w["b_ple_gate"][0])
    put("qw", w["q_norm_w"][0]); put("kw", w["k_norm_w"][0])
    for t in range(3):
        put("cw%d" % t, w["ssm_conv_w"][0][t]); put("fw%d" % t, w["ffn_conv_w"][0][t])
    put("cb", w["ssm_conv_b"][0]); put("fb", w["ffn_conv_b"][0])
    put("dtb", w["dt_bias"][0].reshape(-1)); put("alog", w["a_log"][0].reshape(-1))
    o["cvec"] = cv
    o["rows"] = np.concatenate([w["norm1_w"][0], w["ssd_norm_w"][0], w["d_skip"][0]]).astype(np.float32)[None, :]
    o["consts"] = _consts()
    return o


_CACHE = {}


def run_seqs(xs, ps, w, S, debug=False, n_cores=None):
    if (S, debug) not in _CACHE:
        _CACHE[(S, debug)] = build(S, debug)
    nc, _ = _CACHE[(S, debug)]
    base = _pack_weights(w)
    base["cosT"], base["sinT"] = _rope_tables(S)
    in_maps = []
    for x, p in zip(xs, ps):
        m = dict(base)
        m["x"] = np.ascontiguousarray(x, dtype=np.float32)
        m["p"] = np.ascontiguousarray(p, dtype=np.float32)
        in_maps.append(m)
    res = run_bass_kernel_spmd(nc, in_maps, core_ids=list(range(len(in_maps))))
    return res.results


def kernel(**inputs):
    inp = {k: np.asarray(v) for k, v in inputs.items()}
    S = inp["x_prompt"].shape[1]
    xs = [inp["x_prompt"][b] for b in range(4)] + [inp["x_sample"][b] for b in range(2)]
    ps = [inp["p_prompt"][0, b] for b in range(4)] + [inp["p_sample"][0, b] for b in range(2)]
    xs += [xs[0], xs[1]]
    ps += [ps[0], ps[1]]
    res = run_seqs(xs, ps, inp, S)
    y_prompt = np.stack([res[b]["y"] for b in range(4)]).astype(np.float32)
    y_sample = np.stack([res[4 + b]["y"] for b in range(2)]).astype(np.float32)
    return (y_prompt, y_sample)
```

```python
import numpy as np
from contextlib import ExitStack
import concourse.bass as bass
import concourse.mybir as mybir
from concourse.bass_utils import run_bass_kernel_spmd
from concourse.alu_op_type import AluOpType as ALU

F32 = mybir.dt.float32
BF16 = mybir.dt.bfloat16
AF = mybir.ActivationFunctionType
ENGS = ("pe", "act", "dve", "pool", "sp")
EPS = 1e-6
D = 1024


class Buf:
    __slots__ = ("name", "w", "r")

    def __init__(self, name):
        self.name = name
        self.w = None
        self.r = []


class Op:
    __slots__ = ("eng", "fn", "deps", "sig", "semkey", "val", "is_dma")

    def __init__(self, eng, fn, is_dma, semkey):
        self.eng = eng
        self.fn = fn
        self.deps = []
        self.sig = False
        self.semkey = semkey
        self.val = None
        self.is_dma = is_dma


class T:
    __slots__ = ("ap", "buf")

    def __init__(self, ap, buf):
        self.ap = ap
        self.buf = buf

    def __getitem__(self, idx):
        return T(self.ap[idx], self.buf)

    def v(self, fn):
        return T(fn(self.ap), self.buf)


class Prog:
    def __init__(self, nc):
        self.nc = nc
        self.ops = {e: [] for e in ENGS}
        self.bufs = {}
        self.dmas = []

    def buf(self, name):
        b = self.bufs.get(name)
        if b is None:
            b = self.bufs[name] = Buf(name)
        return b

    def _bl(self, xs):
        out = []
        for x in xs:
            if x is None or isinstance(x, (int, float)):
                continue
            if isinstance(x, (list, tuple)):
                out.extend(self._bl(x))
            elif isinstance(x, T):
                out.append(self.buf(x.buf))
            elif isinstance(x, str):
                out.append(self.buf(x))
        return out

    def op(self, eng, fn, reads=(), writes=(), dma=None, extra=()):
        is_dma = dma is not None
        o = Op(eng, fn, is_dma, dma if is_dma else eng)
        R = self._bl(reads)
        W = self._bl(writes)
        deps = {}
        for b in R:
            if b.w is not None:
                deps[id(b.w)] = b.w
            if b.name.startswith("ps") and not is_dma:
                for r in b.r:
                    if r.eng != eng:
                        deps[id(r)] = r
        for b in W:
            if b.w is not None:
                deps[id(b.w)] = b.w
            for r in b.r:
                deps[id(r)] = r
        for d in extra:
            deps[id(d)] = d
        for d in deps.values():
            if d is o or d.fn is None:
                continue
            if (not d.is_dma) and (not is_dma) and d.eng == eng and fn is not None:
                if eng == "pe":
                    continue
                if not any(b.w is d for b in R):
                    continue
            o.deps.append(d)
            d.sig = True
        for b in R:
            b.r.append(o)
        for b in W:
            b.w = o
            b.r = []
        self.ops[eng].append(o)
        if is_dma:
            self.dmas.append(o)
        return o

    def barrier(self, dummies):
        dm = list(self.dmas)
        self.dmas = []
        self.op("act", lambda e: e.activation(out=dummies["act"], in_=dummies["act"], func=AF.Copy), writes=["bar_act"])
        self.op("dve", lambda e: e.memset(dummies["dve"], 0.0), writes=["bar_dve"])
        self.op("pool", lambda e: e.memset(dummies["pool"], 0.0), writes=["bar_pool"])
        for e in ENGS:
            self.op(e, None, reads=["bar_act", "bar_dve", "bar_pool"], extra=dm)

    def emit(self):
        nc = self.nc
        counts = {}
        keys = []
        for e in ENGS:
            for o in self.ops[e]:
                if o.fn is None:
                    continue
                if o.is_dma:
                    o.sig = True
                if o.sig:
                    k = o.semkey
                    if k not in counts:
                        counts[k] = 0
                        keys.append(k)
                    counts[k] += 16 if o.is_dma else 1
                    o.val = counts[k]
        final = list(self.dmas)
        with ExitStack() as st:
            sems = {k: st.enter_context(nc.semaphore("s_" + str(k))) for k in keys}
            block = st.enter_context(nc.Block())
            handles = {"pe": "tensor", "act": "scalar", "dve": "vector", "pool": "gpsimd", "sp": "sync"}

            def run(ename, eng):
                waited = {}

                def wait_for(dl):
                    need = {}
                    for d in dl:
                        if d.val > need.get(d.semkey, 0):
                            need[d.semkey] = d.val
                    for k, v in need.items():
                        if waited.get(k, 0) < v:
                            eng.wait_ge(sems[k], v)
                            waited[k] = v

                for o in self.ops[ename]:
                    wait_for(o.deps)
                    if o.fn is None:
                        continue
                    ins = o.fn(eng)
                    if o.sig:
                        ins.then_inc(sems[o.semkey], 16 if o.is_dma else 1)
                if ename == "sp":
                    wait_for(final)

            for ename in ENGS:
                def mk(ename=ename):
                    def _f(eng):
                        run(ename, eng)
                    return _f
                getattr(block, handles[ename])(mk())
        self.n_ops = {e: len(self.ops[e]) for e in ENGS}


class Arena:
    def __init__(self, t, nwords):
        self.t = t
        self.n = nwords
        self.off = 0

    def f32(self, n):
        assert self.off + n <= self.n, ("arena overflow", self.off, n, self.n)
        ap = self.t[:, self.off:self.off + n]
        self.off += n
        return ap

    def bf16(self, n):
        nw = (n + 1) // 2
        assert self.off + nw <= self.n, ("arena overflow", self.off, nw, self.n)
        ap = self.t[:, self.off:self.off + nw].bitcast(BF16)
        self.off += nw
        return ap


def r3(ap, a):
    return ap.rearrange("p (a b) -> p a b", a=a)


WSPEC = [
    ("w_q", 1024, 1024, 512), ("w_k", 1024, 256, None), ("w_v", 1024, 256, None), ("w_z", 1024, 2048, None), ("w_xbc", 1024, 3072, 512),
    ("w_dt", 1024, 64, None), ("w_g", 1024, 2048, 128), ("w_a", 1024, 1024, 128), ("w_s", 2048, 1024, 128), ("w_o", 1024, 1024, 512),
    ("w_up", 1024, 5632, 128), ("w_dn", 2816, 1024, 128), ("w_pl", 256, 1024, 128), ("w_pg", 1024, 1024, 128),
]
WMW = {n: mw for n, k, m, mw in WSPEC}


def wshape(k, m, mw):
    return [k, m] if mw is None else [(m // mw) * 128, (k // 128) * mw]


NCONST = 9
CV = {}
_cv_off = 0
for _n, _w in [("n1", 8), ("n2", 8), ("nf", 8), ("gb", 16), ("bpg", 8), ("qw", 1), ("kw", 1), ("cw0", 24), ("cw1", 24), ("cw2", 24),
               ("cb", 24), ("fw0", 44), ("fw1", 44), ("fw2", 44), ("fb", 44), ("dtb", 1), ("alog", 1)]:
    CV[_n] = (_cv_off, _w)
    _cv_off += _w
NCV = _cv_off


def build(S, debug=False):
    NT = S // 128
    TT = 512 if S >= 512 else S
    NTT = S // TT
    CPT = TT // 128
    nc = bass.Bass("TRN2", target_bir_lowering=False)
    P = Prog(nc)
    dt_in = lambda n, sh, d=F32: nc.dram_tensor(n, sh, d, kind="ExternalInput").ap()
    okind = "ExternalOutput" if debug else "Internal"
    dt_sc = lambda n, sh, d: nc.dram_tensor(n, sh, d, kind=okind).ap()
    x_d = dt_in("x", [S, D])
    p_d = dt_in("p", [S, 256])
    wsrc = {n: dt_in(n, wshape(k, m, mw)) for n, k, m, mw in WSPEC}
    consts_d = dt_in("consts", [128, NCONST, 128])
    cv_d = dt_in("cvec", [128, NCV])
    rows_d = dt_in("rows", [1, 1024 + 2048 + 32])
    cos_d = dt_in("cosT", [128, S])
    sin_d = dt_in("sinT", [128, S])
    y_d = nc.dram_tensor("y", [S, D], F32, kind="ExternalOutput").ap()
    wb = {n: dt_sc(n + "_b", wshape(k, m, mw), BF16) for n, k, m, mw in WSPEC}
    hT_d = dt_sc("hT_d", [128, 8, S + 2], BF16)
    xT_d = dt_sc("xT_d", [128, 8, S], F32)
    pT_d = dt_sc("pT_d", [128, 2, S], BF16)
    xtok_d = dt_sc("xtok_d", [S, 2048], BF16)
    btok_d = dt_sc("btok_d", [S, 512], BF16)
    bT_d = dt_sc("bT_d", [128, 4, S], BF16)
    cT_d = dt_sc("cT_d", [128, 4, S], BF16)
    dd_d = dt_sc("dd_d", [S, 128], F32)
    yf_d = dt_sc("yf_d", [S, 2048], F32)
    ssdT_d = dt_sc("ssdT_d", [128, 16, S], BF16)
    x1T_d = dt_sc("x1T_d", [128, 8, S], F32)
    h2T_d = dt_sc("h2T_d", [128, 8, S + 2], BF16)
    rs_d = nc.dram_tensor("rs_d", [4, 512], F32, kind="Internal").ap()

    with ExitStack() as st:
        NW_ALL = 51000
        sb_t = st.enter_context(nc.sbuf_tensor("sb", [128, NW_ALL], F32))
        ps_t = st.enter_context(nc.psum_tensor("psum", [128, 4096], F32))
        RES = Arena(sb_t, NW_ALL)
        PS = [T(ps_t[:, b * 512:(b + 1) * 512], "ps%d" % b) for b in range(8)]
        psbf = lambda b: T(ps_t[:, b * 512:(b + 1) * 512].bitcast(BF16), "ps%d" % b)
        ps2 = lambda b: T(ps_t[:, b * 512:(b + 2) * 512], "ps%d" % b)

        def bufs(*xs):
            return [x for x in xs if isinstance(x, (T, str))]

        def apof(x):
            return x.ap if isinstance(x, T) else x

        def mm(out, lhsT, rhs, start=True, stop=True, extra_w=()):
            P.op("pe", lambda e: e.matmul(out.ap, lhsT.ap, rhs.ap, start=start, stop=stop), reads=[lhsT, rhs], writes=[out, *extra_w])

        def tr(out, in_, ident, extra_w=()):
            P.op("pe", lambda e: e.transpose(out.ap, in_.ap, ident.ap), reads=[in_, ident], writes=[out, *extra_w])

        def act(out, in_, func, scale=None, bias=None, accum=None, extra_r=(), extra_w=()):
            kw = {}
            if scale is not None:
                kw["scale"] = apof(scale)
            if bias is not None:
                kw["bias"] = apof(bias)
            if accum is not None:
                kw["accum_out"] = accum.ap
            P.op("act", lambda e: e.activation(out=out.ap, in_=in_.ap, func=func, **kw),
                 reads=[in_, *bufs(scale, bias), *extra_r], writes=[out, *bufs(accum), *extra_w])

        def tt(eng, out, a, b, op, extra_r=(), extra_w=()):
            P.op(eng, lambda e: e.tensor_tensor(out=out.ap, in0=a.ap, in1=b.ap, op=op), reads=[a, b, *extra_r], writes=[out, *extra_w])

        def stt(eng, out, a, scalar, b, op0, op1, extra_r=()):
            P.op(eng, lambda e: e.scalar_tensor_tensor(out=out.ap, in0=a.ap, scalar=apof(scalar), in1=b.ap, op0=op0, op1=op1),
                 reads=[a, b, *bufs(scalar), *extra_r], writes=[out])

        def ts(eng, out, a, s1, s2, op0, op1=None, extra_r=()):
            if op1 is None:
                P.op(eng, lambda e: e.tensor_scalar(out=out.ap, in0=a.ap, scalar1=apof(s1), scalar2=None, op0=op0),
                     reads=[a, *bufs(s1), *extra_r], writes=[out])
            else:
                P.op(eng, lambda e: e.tensor_scalar(out=out.ap, in0=a.ap, scalar1=apof(s1), scalar2=apof(s2), op0=op0, op1=op1),
                     reads=[a, *bufs(s1, s2), *extra_r], writes=[out])

        def cp(eng, out, in_, extra_r=(), extra_w=()):
            if eng == "act":
                P.op("act", lambda e: e.copy(out=out.ap, in_=in_.ap), reads=[in_, *extra_r], writes=[out, *extra_w])
            else:
                P.op(eng, lambda e: e.tensor_copy(out=out.ap, in_=in_.ap), reads=[in_, *extra_r], writes=[out, *extra_w])

        def recip(out, in_):
            P.op("dve", lambda e: e.reciprocal(out=out.ap, in_=in_.ap), reads=[in_], writes=[out])

        def memset(eng, out, val):
            P.op(eng, lambda e: e.memset(out.ap, val), writes=[out])

        def dma(out, in_, key, eng="sp", slow=False):
            return P.op(eng, lambda e: e.dma_start(out=apof(out), in_=apof(in_), allow_slow_non_contiguous=slow), reads=[in_], writes=[out], dma=key)

        cst_f = T(r3(RES.f32(NCONST * 128), NCONST), "cst_f")
        cst_b = T(r3(RES.bf16(NCONST * 128), NCONST), "cst_b")
        cvec = T(RES.f32(NCV), "cvec")
        acol = T(RES.f32(1), "acol")
        dum = {e: RES.f32(1) for e in ("act", "dve", "pool")}
        ident_f = cst_f[:, 0, :]
        ident_b, perm_b, oblk_b, ones_b, le_b, gt_b, ge_b, lt_b = [cst_b[:, i, :] for i in range(8)]
        sel_f = cst_f[:, 8, :]
        dma(cst_f, consts_d, "c0")
        dma(cvec, cv_d, "c1")
        cp("dve", cst_b, cst_f)
        cvs = lambda n, i=0: cvec[:, CV[n][0] + i:CV[n][0] + i + 1]
        act(acol[0:64, :], cvs("alog")[0:64, :], AF.Exp)
        ts("dve", acol[0:64, :], acol[0:64, :], -1.0, None, ALU.mult)
        const_mark = RES.off
        KT = [T(RES.bf16(S), "KT%d" % i) for i in range(2)]
        VA_flat = T(RES.bf16(NT * 4 * 65), "VA")
        VA = VA_flat.v(lambda a: a[:, 0:NT * 4 * 65].rearrange("p (c g e) -> p c g e", c=NT, g=4))
        res_mark = RES.off

        def phase_arena(early=False):
            a = Arena(sb_t, NW_ALL)
            a.off = const_mark if early else res_mark
            return a

        for n, k, m, mw in WSPEC:
            nr = wshape(k, m, mw)[0]
            rows = 512 if nr >= 512 else nr
            for r0 in range(0, nr, rows):
                r1 = min(nr, r0 + rows)
                dma(T(wb[n][r0:r1, :], "wb_" + n), T(wsrc[n][r0:r1, :], "wsrc_" + n), "wc", eng="pool")

        class WStream:
            def __init__(self, A, nslots=3):
                self.slots = [T(A.bf16(4096), "wslot%d" % i) for i in range(nslots)]
                self.ns = nslots
                self.specs = []
                self.idx = 0
                self.loaded = 0

            def plan(self, specs):
                self.specs.extend(specs)

            def _load(self, i):
                n, kc, m0, mw = self.specs[i]
                sl = self.slots[i % self.ns]
                assert WMW[n] == mw and m0 % mw == 0, (n, mw, m0)
                mt = m0 // mw
                src = T(wb[n][mt * 128:(mt + 1) * 128, :], "wb_" + n)
                dma(sl.v(lambda a: a[:, 0:kc * mw]), src, "w%d" % (i % self.ns))

            def next(self, name):
                while self.loaded < min(len(self.specs), self.idx + self.ns):
                    self._load(self.loaded)
                    self.loaded += 1
                n, kc, m0, mw = self.specs[self.idx]
                assert n == name, (n, name)
                sl = self.slots[self.idx % self.ns]
                self.idx += 1
                return sl.v(lambda a: r3(a[:, 0:kc * mw], kc))

        def rms_cols(A_sq, src_fn, nck, pb, out_rinv, n, scratch_sd):
            for m in range(nck):
                sq = A_sq[m % 2]
                tt("pool", sq, src_fn(m), src_fn(m), ALU.mult)
                mm(PS[pb][:, 0:n], ones_b, sq, start=(m == 0), stop=(m == nck - 1))
            act(scratch_sd, PS[pb][:, 0:n], AF.Sqrt, scale=1.0 / (128 * nck), bias=EPS)
            recip(out_rinv, scratch_sd)

        P.barrier(dum)
        A = phase_arena(True)
        w1bc = T(A.f32(1024), "w1bc")
        dma(w1bc, T(rows_d[0:1, 0:1024].partition_broadcast(128), "rows"), "c2")
        zt = T(A.bf16(16), "zt")
        memset("dve", zt, 0.0)
        for d_ in (hT_d, h2T_d):
            for col in (0, S + 1):
                dma(T(d_[:, :, col:col + 1], "halo"), zt.v(lambda a: r3(a[:, 0:8], 8)), "c3", slow=True)
        xt = [T(r3(A.f32(CPT * 1024), CPT), "A_xt%d" % i) for i in range(2)]
        pt = [T(r3(A.f32(CPT * 256), CPT), "A_pt%d" % i) for i in range(2)]
        hb = T(r3(A.bf16(CPT * 1024), CPT), "A_hb")
        pbf = T(r3(A.bf16(CPT * 256), CPT), "A_pbf")
        junk = T(A.f32(1024), "A_junk")
        ss = T(A.f32(4), "A_ss")
        sd = T(A.f32(4), "A_sd")
        rstd = T(A.f32(4), "A_rstd")
        hT_sb = T(r3(A.bf16(8 * TT), 8), "A_hT")
        xT_sb = T(r3(A.f32(8 * TT), 8), "A_xT")
        pT_sb = T(r3(A.bf16(2 * TT), 2), "A_pT")

        def loadA(j):
            dma(xt[j % 2], T(x_d[j * TT:(j + 1) * TT, :].rearrange("(c p) d -> p c d", p=128), "x"), "ax%d" % (j % 2))
            dma(pt[j % 2], T(p_d[j * TT:(j + 1) * TT, :].rearrange("(c p) d -> p c d", p=128), "p"), "ap%d" % (j % 2))

        loadA(0)
        for j in range(NTT):
            if j + 1 < NTT:
                loadA(j + 1)
            X = xt[j % 2]
            memset("dve", ss, 0.0)
            for c in range(CPT):
                act(junk, X[:, c, :], AF.Square, accum=ss[:, c:c + 1])
            act(sd[:, 0:CPT], ss[:, 0:CPT], AF.Sqrt, scale=1.0 / D, bias=EPS)
            recip(rstd[:, 0:CPT], sd[:, 0:CPT])
            for c in range(CPT):
                stt("dve", hb[:, c, :], X[:, c, :], rstd[:, c:c + 1], w1bc, ALU.mult, ALU.mult)
                cp("pool", pbf[:, c, :], pt[j % 2][:, c, :])
            for c in range(CPT):
                pb = c % 2
                for kc in range(8):
                    tr(psbf(pb)[:, kc * 128:(kc + 1) * 128], hb[:, c, kc * 128:(kc + 1) * 128], ident_b)
                cp("act", hT_sb[:, :, c * 128:(c + 1) * 128], psbf(pb).v(lambda a: r3(a, 8)))
                xb = 2 + 2 * (c % 2)
                for kc in range(8):
                    tr(PS[xb + kc // 4][:, (kc % 4) * 128:(kc % 4 + 1) * 128], X[:, c, kc * 128:(kc + 1) * 128], ident_f)
                for hh in range(2):
                    cp("dve", xT_sb[:, 4 * hh:4 * hh + 4, c * 128:(c + 1) * 128], PS[xb + hh].v(lambda a: r3(a, 4)))
                for kc in range(2):
                    tr(psbf(6 + c % 2)[:, kc * 128:(kc + 1) * 128], pbf[:, c, kc * 128:(kc + 1) * 128], ident_b)
                cp("act", pT_sb[:, :, c * 128:(c + 1) * 128], psbf(6 + c % 2)[:, 0:256].v(lambda a: r3(a, 2)))
            dma(T(hT_d[:, :, 1 + j * TT:1 + (j + 1) * TT], "hT_d%d" % j), hT_sb, "ao0")
            dma(T(xT_d[:, :, j * TT:(j + 1) * TT], "xT_d%d" % j), xT_sb, "ao1")
            dma(T(pT_d[:, :, j * TT:(j + 1) * TT], "pT_d%d" % j), pT_sb, "ao2")
        P.barrier(dum)

        def conv3(acc, main, halo, w0, w1, w2, b):
            n = TT
            act(acc, main, AF.Identity, scale=w1, bias=b)
            stt("dve", acc[:, 1:n], main[:, 0:n - 1], w0, acc[:, 1:n], ALU.mult, ALU.add)
            stt("dve", acc[:, 0:n - 1], main[:, 1:n], w2, acc[:, 0:n - 1], ALU.mult, ALU.add)
            stt("dve", acc[:, 0:1], halo[:, 0:1], w0, acc[:, 0:1], ALU.mult, ALU.add)
            stt("dve", acc[:, n - 1:n], halo[:, 1:2], w2, acc[:, n - 1:n], ALU.mult, ALU.add)

        def load_hTw(dst, src_d, j, key):
            dma(dst, T(src_d[:, :, j * TT:j * TT + TT + 2], "hTsrc"), key)

        A = phase_arena(True)
        WS = WStream(A)
        hTw = [T(r3(A.bf16(8 * (TT + 2)), 8), "B_hTw%d" % i) for i in range(2)]
        wdt = T(r3(A.bf16(8 * 64), 8), "B_wdt")
        dma(wdt, T(wb["w_dt"].rearrange("(kc p) m -> p kc m", p=128), "wb_w_dt"), "c4")
        xTc = T(r3(A.bf16(16 * TT), 16), "B_xTc")
        bTc = T(r3(A.bf16(4 * TT), 4), "B_bTc")
        cTc = T(r3(A.bf16(4 * TT), 4), "B_cTc")
        acc = [T(A.f32(TT), "B_acc%d" % i) for i in range(2)]
        ddT = T(A.f32(TT), "B_ddT")
        xtok = [T(A.bf16(2048), "B_xtok%d" % i) for i in range(2)]
        btok = [T(A.bf16(512), "B_btok%d" % i) for i in range(2)]
        ddtok = [T(A.f32(128), "B_ddtok%d" % i) for i in range(2)]
        for j in range(NTT):
            WS.plan([("w_xbc", 8, g * 512, 512) for g in range(6)])
        load_hTw(hTw[0], hT_d, 0, "bh0")
        for j in range(NTT):
            if j + 1 < NTT:
                load_hTw(hTw[(j + 1) % 2], hT_d, j + 1, "bh%d" % ((j + 1) % 2))
            H = hTw[j % 2]
            for g in range(6):
                wt = WS.next("w_xbc")
                for mi in range(4):
                    m = g * 4 + mi
                    pb = m % 2
                    for kc in range(8):
                        mm(PS[pb][:, 0:TT], wt[:, kc, mi * 128:(mi + 1) * 128], H[:, kc, 1:TT + 1], start=(kc == 0), stop=(kc == 7))
                    for kc in range(8):
                        mm(PS[2 + pb][:, 0:2], wt[:, kc, mi * 128:(mi + 1) * 128], H[:, kc, 0:TT + 2:TT + 1], start=(kc == 0), stop=(kc == 7))
                    ac = acc[m % 2]
                    conv3(ac, PS[pb][:, 0:TT], PS[2 + pb][:, 0:2], cvs("cw0", m), cvs("cw1", m), cvs("cw2", m), cvs("cb", m))
                    dst = xTc[:, m, :] if m < 16 else (bTc[:, m - 16, :] if m < 20 else cTc[:, m - 20, :])
                    act(dst, ac, AF.Silu)
            for kc in range(8):
                mm(PS[4][0:64, 0:TT], wdt[:, kc, :], H[:, kc, 1:TT + 1], start=(kc == 0), stop=(kc == 7))
            act(ddT[0:64, :], PS[4][0:64, 0:TT], AF.Exp, bias=cvs("dtb")[0:64, :])
            act(ddT[0:64, :], ddT[0:64, :], AF.Ln, bias=1.0)
            ts("dve", ddT[64:128, :], ddT[0:64, :], acol[0:64, :], None, ALU.mult)
            dma(T(bT_d[:, :, j * TT:(j + 1) * TT], "bT_d%d" % j), bTc, "bo0")
            dma(T(cT_d[:, :, j * TT:(j + 1) * TT], "cT_d%d" % j), cTc, "bo1")
            for c in range(CPT):
                cg = j * CPT + c
                q = c % 2
                for m in range(16):
                    tr(psbf(5 + m // 8)[:, (m % 8) * 128:(m % 8 + 1) * 128], xTc[:, m, c * 128:(c + 1) * 128], ident_b)
                cp("act", xtok[q][:, 0:1024], psbf(5))
                cp("dve", xtok[q][:, 1024:2048], psbf(6))
                for m in range(4):
                    tr(psbf(7)[:, m * 128:(m + 1) * 128], bTc[:, m, c * 128:(c + 1) * 128], ident_b)
                cp("act", btok[q], psbf(7)[:, 0:512])
                tr(PS[4][:, 0:128], ddT[:, c * 128:(c + 1) * 128], ident_f)
                cp("dve", ddtok[q], PS[4][:, 0:128])
                dma(T(xtok_d[cg * 128:(cg + 1) * 128, :], "xtok_d%d" % cg), xtok[q], "bo2%d" % q)
                dma(T(btok_d[cg * 128:(cg + 1) * 128, :], "btok_d%d" % cg), btok[q], "bo3%d" % q)
                dma(T(dd_d[cg * 128:(cg + 1) * 128, :], "dd_d%d" % cg), ddtok[q], "bo4%d" % q)
        P.barrier(dum)

        A = phase_arena(True)
        s_x = [T(A.bf16(2048), "S_x%d" % i) for i in range(2)]
        s_b = [T(A.bf16(512), "S_b%d" % i) for i in range(2)]
        s_bT = [T(r3(A.bf16(512), 4), "S_bT%d" % i) for i in range(2)]
        s_cT = [T(r3(A.bf16(512), 4), "S_cT%d" % i) for i in range(2)]
        s_dd = [T(A.f32(128), "S_dd%d" % i) for i in range(2)]
        s_adtb = T(A.bf16(32), "S_adtb")
        s_E = T(A.f32(96), "S_E")
        s_xdt = T(A.bf16(2048), "S_xdt")
        s_xdtA = T(A.bf16(2048), "S_xdtA")
        s_at = [T(A.bf16(512), "S_at%d" % i) for i in range(3)]
        s_dec = [T(A.bf16(512), "S_dec%d" % i) for i in range(3)]
        s_M = [T(A.bf16(512), "S_M%d" % i) for i in range(3)]
        s_cbm = T(A.bf16(512), "S_cbm")
        s_h = T(A.f32(2048), "S_h")
        s_hb = T(A.bf16(2048), "S_hb")
        s_tmp = T(A.f32(512), "S_tmp")
        s_y = [T(A.f32(2048), "S_y%d" % i) for i in range(2)]
        s_wz = T(r3(A.bf16(8 * 2048), 8), "S_wz")
        s_hT = [T(r3(A.bf16(8 * 128), 8), "S_hT%d" % i) for i in range(3)]
        s_yf = [T(A.f32(2048), "S_yf%d" % i) for i in range(2)]
        s_nw = T(A.f32(2048), "S_nw")
        s_dk = T(A.f32(32), "S_dk")
        s_sz = [T(A.f32(512), "S_sz%d" % i) for i in range(2)]
        s_zs = [T(A.f32(512), "S_zs%d" % i) for i in range(2)]
        s_gs = T(A.f32(4), "S_gs")
        s_gsd = T(A.f32(4), "S_gsd")
        s_gr = T(A.f32(4), "S_gr")
        s_junk = T(A.f32(512), "S_junk")
        s_sb = T(A.bf16(2048), "S_sb")
        s_sT = T(r3(A.bf16(16 * 128), 16), "S_sT")
        dma(s_wz, T(wb["w_z"].rearrange("(kc p) m -> p kc m", p=128), "wb_w_z"), "c5")
        dma(s_nw, T(rows_d[0:1, 1024:3072].partition_broadcast(128), "rows"), "c6")
        dma(s_dk, T(rows_d[0:1, 3072:3104].partition_broadcast(128), "rows"), "c7")
        bc_h = lambda t, n: t.v(lambda a: a.unsqueeze(2).to_broadcast([128, n, 64]))
        s_did = T(r3(A.bf16(32 * 128), 32), "S_did")
        for h in range(32):
            ts("dve", s_did[:, h, :], ident_b, s_dk[:, h:h + 1], None, ALU.mult)

        def ssd_load(c, q, di, t3=0):
            dma(s_x[q], T(xtok_d[c * 128:(c + 1) * 128, :], "xtok_d%d" % c), "sl0%d" % q)
            dma(s_b[q], T(btok_d[c * 128:(c + 1) * 128, :], "btok_d%d" % c), "sl1%d" % q)
            dma(s_bT[q], T(bT_d[:, :, c * 128:(c + 1) * 128], "bT_d%d" % (c // CPT)), "sl2%d" % q)
            dma(s_cT[q], T(cT_d[:, :, c * 128:(c + 1) * 128], "cT_d%d" % (c // CPT)), "sl3%d" % q)
            dma(s_dd[q], T(dd_d[c * 128:(c + 1) * 128, :], "dd_d%d" % c), "sl4%d" % q)
            if di == 1:
                dma(s_yf[q], T(yf_d[c * 128:(c + 1) * 128, :], "yf_d%d" % c), "sl5%d" % q)
                dma(s_hT[t3], T(hT_d[:, :, 1 + c * 128:1 + (c + 1) * 128], "hT_d%d" % (c // CPT)), "sl6%d" % t3)

        for di in range(2):
            tri, U, mask = (le_b, gt_b, le_b) if di == 0 else (ge_b, lt_b, ge_b)
            order = list(range(NT)) if di == 0 else list(range(NT - 1, -1, -1))
            memset("dve", s_h, 0.0)
            memset("pool", s_hb, 0.0)
            ssd_load(order[0], 0, di, 0)
            pend_tail = []
            for it, c in enumerate(order):
                q = it % 2
                if it + 1 < NT:
                    ssd_load(order[it + 1], (it + 1) % 2, di, (it + 1) % 3)
                dtc = s_dd[q][:, di * 32:(di + 1) * 32]
                adt = s_dd[q][:, 64 + di * 32:64 + (di + 1) * 32]
                cp("pool", s_adtb, adt)
                mm(PS[0][:, 0:32], tri, s_adtb)
                mm(PS[0][:, 32:64], U, s_adtb)
                mm(PS[0][:, 64:96], ones_b, s_adtb)
                act(s_E, PS[0][:, 0:96], AF.Exp)
                eac, dA, cd = s_E[:, 0:32], s_E[:, 32:64], s_E[:, 64:96]
                X3 = s_x[q].v(lambda a: r3(a, 32))
                for g in range(4):
                    mm(PS[1][:, g * 128:(g + 1) * 128], s_bT[q][:, g, :], s_cT[q][:, g, :])
                tt("dve", s_cbm.v(lambda a: r3(a, 4)), PS[1].v(lambda a: r3(a, 4)),
                   mask.v(lambda a: a.unsqueeze(1).to_broadcast([128, 4, 128])), ALU.mult)
                Y = s_y[q]
                hgs = [(g, hh) for g in range(4) for hh in range(2)]

                SEGB = (2, 3, 7)

                def stage1(i):
                    g, hh = hgs[i]
                    k = i % 3
                    h0 = g * 8 + hh * 4
                    tt("pool" if i % 2 == 0 else "dve", s_at[k].v(lambda a: r3(a, 4)),
                       tri.v(lambda a: a.unsqueeze(1).to_broadcast([128, 4, 128])),
                       adt[:, h0:h0 + 4].v(lambda a: a.unsqueeze(2).to_broadcast([128, 4, 128])), ALU.mult)
                    mm(PS[SEGB[k]], U, s_at[k])
                    act(s_dec[k], PS[SEGB[k]], AF.Exp)
                    tt("dve", s_M[k].v(lambda a: r3(a, 4)), s_dec[k].v(lambda a: r3(a, 4)),
                       s_cbm[:, g * 128:(g + 1) * 128].v(lambda a: a.unsqueeze(1).to_broadcast([128, 4, 128])), ALU.mult)

                def stage2(i):
                    g, hh = hgs[i]
                    k = i % 3
                    h0 = g * 8 + hh * 4
                    for hi in range(4):
                        h = h0 + hi
                        oy = PS[4 + g % 2][:, (hh * 4 + hi) * 64:(hh * 4 + hi + 1) * 64]
                        mm(oy, s_M[k][:, hi * 128:(hi + 1) * 128], s_xdt[:, h * 64:(h + 1) * 64], start=True, stop=(di == 0))
                        if di == 1:
                            mm(oy, s_did[:, h, :], s_x[q][:, h * 64:(h + 1) * 64], start=False, stop=True)
                    if hh == 1:
                        mm(PS[6], s_cT[q][:, g, :], s_hb[:, g * 512:(g + 1) * 512])
                        tt("dve", s_tmp.v(lambda a: r3(a, 8)), PS[6].v(lambda a: r3(a, 8)), bc_h(eac[:, g * 8:(g + 1) * 8], 8), ALU.mult)
                        if di == 1:
                            tt("dve", s_tmp, s_tmp, s_yf[q][:, g * 512:(g + 1) * 512], ALU.add)
                        tt("dve", Y[:, g * 512:(g + 1) * 512], PS[4 + g % 2], s_tmp, ALU.add)

                stage1(0)
                stage1(1)
                for g_ in range(4):
                    tt("pool", s_xdt[:, g_ * 512:(g_ + 1) * 512].v(lambda a: r3(a, 8)), s_x[q][:, g_ * 512:(g_ + 1) * 512].v(lambda a: r3(a, 8)),
                       bc_h(dtc[:, g_ * 8:(g_ + 1) * 8], 8), ALU.mult)
                for i in range(8):
                    if i + 2 < 8:
                        stage1(i + 2)
                    stage2(i)
                    if pend_tail:
                        for f_ in pend_tail.pop(0):
                            f_()
                    if i in (1, 2, 3, 4):
                        g_ = i - 1
                        tt("pool", s_xdtA[:, g_ * 512:(g_ + 1) * 512].v(lambda a: r3(a, 8)), s_xdt[:, g_ * 512:(g_ + 1) * 512].v(lambda a: r3(a, 8)),
                           bc_h(dA[:, g_ * 8:(g_ + 1) * 8], 8), ALU.mult)
                    if i in (4, 5, 6, 7):
                        g_ = i - 4
                        tt("pool", s_h[:, g_ * 512:(g_ + 1) * 512].v(lambda a: r3(a, 8)), s_h[:, g_ * 512:(g_ + 1) * 512].v(lambda a: r3(a, 8)),
                           bc_h(cd[:, g_ * 8:(g_ + 1) * 8], 8), ALU.mult)
                for g in range(4):
                    mm(PS[g % 2], s_b[q][:, g * 128:(g + 1) * 128], s_xdtA[:, g * 512:(g + 1) * 512])
                    tt("dve", s_h[:, g * 512:(g + 1) * 512], PS[g % 2], s_h[:, g * 512:(g + 1) * 512], ALU.add)
                cp("act", s_hb, s_h)
                if di == 0:
                    dma(T(yf_d[c * 128:(c + 1) * 128, :], "yf_d%d" % c), Y, "so0%d" % q)
                else:
                    while pend_tail:
                        for f_ in pend_tail.pop(0):
                            f_()
                    def make_tail(c=c, Y=Y, hT=s_hT[it % 3]):
                        def z_mm(g):
                            for kc in range(8):
                                mm(PS[g % 2], hT[:, kc, :], s_wz[:, kc, g * 512:(g + 1) * 512], start=(kc == 0), stop=(kc == 7))
                        def f_silu(g):
                            act(s_sz[g % 2], PS[g % 2], AF.Tanh, scale=0.5)
                            act(s_zs[g % 2], PS[g % 2], AF.Identity, scale=0.5)

                        def f_mul(g):
                            Yg = Y[:, g * 512:(g + 1) * 512]
                            stt("dve", Yg, s_sz[g % 2], 1.0, Yg, ALU.add, ALU.mult)
                            tt("pool", Yg, Yg, s_zs[g % 2], ALU.mult)
                        f_sq = lambda g: act(s_junk, Y[:, g * 512:(g + 1) * 512], AF.Square, accum=s_gs[:, g:g + 1])
                        slots = []
                        for t in range(7):
                            sl = []
                            if t == 0:
                                sl.append(lambda: memset("dve", s_gs, 0.0))
                            for fn_, off in ((z_mm, 0), (f_silu, 1), (f_mul, 2), (f_sq, 3)):
                                g = t - off
                                if 0 <= g < 4:
                                    sl.append(lambda fn_=fn_, g=g: fn_(g))
                            slots.append(sl)
                        slots[6].append(lambda: act(s_gsd, s_gs, AF.Sqrt, scale=1.0 / 512, bias=EPS))

                        def fin_a():
                            recip(s_gr, s_gsd)
                            for g in range(4):
                                stt("dve", s_sb[:, g * 512:(g + 1) * 512], Y[:, g * 512:(g + 1) * 512], s_gr[:, g:g + 1],
                                    s_nw[:, g * 512:(g + 1) * 512], ALU.mult, ALU.mult)

                        def fin_b():
                            for m in range(16):
                                tr(psbf(0 if m < 8 else 1)[:, (m % 8) * 128:(m % 8 + 1) * 128], s_sb[:, m * 128:(m + 1) * 128], ident_b)
                            cp("act", s_sT[:, 0:8, :], psbf(0).v(lambda a: r3(a, 8)))
                            cp("dve", s_sT[:, 8:16, :], psbf(1).v(lambda a: r3(a, 8)))
                            dma(T(ssdT_d[:, :, c * 128:(c + 1) * 128], "ssdT_d%d" % (c // CPT)), s_sT, "so1")
                        slots.append([fin_a])
                        slots.append([fin_b])
                        return slots
                    pend_tail = make_tail()
            while pend_tail:
                for f_ in pend_tail.pop(0):
                    f_()
            P.barrier(dum)

        A = phase_arena()
        hTw = [T(r3(A.bf16(8 * (TT + 2)), 8), "K_hTw%d" % i) for i in range(2)]
        wk = T(r3(A.bf16(8 * 256), 8), "K_wk")
        wv = T(r3(A.bf16(8 * 256), 8), "K_wv")
        dma(wk, T(wb["w_k"].rearrange("(kc p) m -> p kc m", p=128), "wb_w_k"), "c8")
        dma(wv, T(wb["w_v"].rearrange("(kc p) m -> p kc m", p=128), "wb_w_v"), "c9")
        cs = [[T(A.f32(TT), "K_cs%d%d" % (i, k)) for k in range(2)] for i in range(2)]
        memset("dve", VA_flat, 1.0)

        class Rope:
            def __init__(self, A, pfx):
                self.sq = T(A.bf16(TT), pfx + "sq")
                self.sd = T(A.f32(TT), pfx + "sd")
                self.ri = T(A.f32(TT), pfx + "ri")
                self.kn = T(A.bf16(TT), pfx + "kn")
                self.t1 = T(A.f32(TT), pfx + "t1")
                self.t2 = T(A.f32(TT), pfx + "t2")

            def run(self, src_ps, wcol, cosT, sinT, dst, pbs, pbs2):
                act(self.sq, src_ps, AF.Square)
                act(self.kn, src_ps, AF.Identity, scale=wcol)
                mm(PS[pbs][:, 0:TT], oblk_b, self.sq)
                mm(PS[pbs2][:, 0:TT], perm_b, self.kn)
                act(self.sd, PS[pbs][:, 0:TT], AF.Ln, scale=1.0 / 64, bias=EPS)
                act(self.ri, self.sd, AF.Exp, scale=-0.5)
                tt("pool", self.t1, self.kn, cosT, ALU.mult)
                tt("dve", self.t2, PS[pbs2][:, 0:TT], sinT, ALU.mult)
                tt("pool", self.t1, self.t1, self.t2, ALU.add)
                tt("dve", dst, self.t1, self.ri, ALU.mult)

        RP = Rope(A, "K_")

        def load_cs(dst, j, key):
            dma(dst[0], T(cos_d[:, j * TT:(j + 1) * TT], "cos"), key + "a")
            dma(dst[1], T(sin_d[:, j * TT:(j + 1) * TT], "sin"), key + "b")

        load_hTw(hTw[0], hT_d, 0, "kh0")
        load_cs(cs[0], 0, "kc0")
        for j in range(NTT):
            if j + 1 < NTT:
                load_hTw(hTw[(j + 1) % 2], hT_d, j + 1, "kh%d" % ((j + 1) % 2))
                load_cs(cs[(j + 1) % 2], j + 1, "kc%d" % ((j + 1) % 2))
            H = hTw[j % 2]
            for pp in range(2):
                for kc in range(8):
                    mm(PS[pp][:, 0:TT], wk[:, kc, pp * 128:(pp + 1) * 128], H[:, kc, 1:TT + 1], start=(kc == 0), stop=(kc == 7))
                RP.run(PS[pp][:, 0:TT], cvs("kw"), cs[j % 2][0], cs[j % 2][1], KT[pp][:, j * TT:(j + 1) * TT], 2 + pp, 4 + pp)
            for c in range(CPT):
                cg = j * CPT + c
                pb = 6 + c % 2
                for kc in range(8):
                    mm(PS[pb][:, 0:256], H[:, kc, 1 + c * 128:1 + (c + 1) * 128], wv[:, kc, :], start=(kc == 0), stop=(kc == 7))
                cp("act" if c % 2 == 0 else "dve", VA[:, cg, :, 0:64], PS[pb][:, 0:256].v(lambda a: r3(a, 4)))
        P.barrier(dum)

        A = phase_arena()
        WS = WStream(A)
        hT1 = T(r3(A.bf16(8 * TT), 8), "E_hT")
        cs1 = [T(A.f32(TT), "E_cs%d" % k) for k in range(2)]
        QT = T(r3(A.bf16(8 * TT), 8), "E_QT")
        attnT = T(r3(A.bf16(8 * TT), 8), "E_attnT")
        PT = [T(A.bf16(2 * TT), "E_PT%d" % i) for i in range(2)]
        ssdT = T(r3(A.bf16(16 * TT), 16), "E_ssdT")
        xT = T(r3(A.f32(8 * TT), 8), "E_xT")
        RQ = Rope(A, "E_")
        RQ2 = [RQ, Rope(A, "E2_")]
        o_sb = T(A.f32(TT), "E_osb")
        o_rec = T(A.f32(TT), "E_orec")
        ga, gs_, m1, m2 = RQ.sd, RQ.ri, RQ.t1, RQ.t2
        sqb = [T(A.bf16(TT), "E_sqb%d" % i) for i in range(2)]
        rinv, sdt = o_sb, o_rec
        scale = 64 ** -0.5
        for j in range(NTT):
            WS.plan([("w_q", 8, g * 512, 512) for g in range(2)])
            for m in range(8):
                WS.plan([("w_a", 8, m * 128, 128), ("w_s", 16, m * 128, 128), ("w_g", 8, m * 128, 128), ("w_g", 8, 1024 + m * 128, 128)])
            WS.plan([("w_o", 8, g * 512, 512) for g in range(2)])
        def loadE_early(j):
            dma(hT1, T(hT_d[:, :, 1 + j * TT:1 + (j + 1) * TT], "hT_d%d" % j), "eh")
            load_cs(cs1, j, "ec")
            dma(ssdT, T(ssdT_d[:, :, j * TT:(j + 1) * TT], "ssdT_d%d" % j), "es")

        loadE_early(0)
        for j in range(NTT):
            dma(xT, T(xT_d[:, :, j * TT:(j + 1) * TT], "xT_d%d" % j), "ex")
            for g in range(2):
                wt = WS.next("w_q")
                for mi in range(4):
                    m = g * 4 + mi
                    pb = m % 2
                    for kc in range(8):
                        mm(PS[pb][:, 0:TT], wt[:, kc, mi * 128:(mi + 1) * 128], hT1[:, kc, :], start=(kc == 0), stop=(kc == 7))
                    RQ2[m % 2].run(PS[pb][:, 0:TT], cvs("qw"), cs1[0], cs1[1], QT[:, m, :], 2 + pb, 4 + pb)
            iters = [(pp, r, kb) for pp in range(2) for r in range(4) for kb in range(NT)]
            bcs = [[o_sb, RQ.t1], [RQ2[1].t1, RQ2[1].t2]]
            orec = [o_rec, RQ.t2]
            npair = [0]

            def emit_qk(i):
                pp, r, kb = iters[i]
                jq = pp * 4 + r
                sb_ = (i % 2) * 2
                pt_ = PT[i % 2]
                mm(PS[sb_][:, 0:TT], KT[pp][0:64, kb * 128:(kb + 1) * 128], QT[0:64, jq, :])
                mm(PS[sb_ + 1][:, 0:TT], KT[pp][64:128, kb * 128:(kb + 1) * 128], QT[64:128, jq, :])
                P.op("act", lambda e, o=pt_.ap, i_=ps_t[:, sb_ * 512:sb_ * 512 + 2 * 512]: e.activation(out=o, in_=i_, func=AF.Exp, scale=scale),
                     reads=[PS[sb_], PS[sb_ + 1]], writes=[pt_])

            emit_qk(0)
            for i in range(len(iters)):
                if i + 1 < len(iters):
                    emit_qk(i + 1)
                pp, r, kb = iters[i]
                pt_ = PT[i % 2]
                par = (i // NT) % 2
                ob = 4 + 2 * par
                mm(PS[ob][0:65, 0:TT], VA[:, kb, 2 * pp, :], pt_[:, 0:TT], start=(kb == 0), stop=(kb == NT - 1))
                mm(PS[ob + 1][0:65, 0:TT], VA[:, kb, 2 * pp + 1, :], pt_[:, TT:2 * TT], start=(kb == 0), stop=(kb == NT - 1))
                if kb == NT - 1:
                    jq = pp * 4 + r
                    for half in range(2):
                        rr = orec[par][64:65, :] if half == 0 else orec[par][32:33, :]
                        recip(rr, PS[ob + half][64:65, 0:TT])
                        rsd = T(rs_d[2 * par + half:2 * par + half + 1, 0:TT], "rs_d%d%d" % (par, half))
                        dma(rsd, rr, "rs%d%d" % (par, half))
                        dma(bcs[par][half][0:64, :], rsd.v(lambda a: a.partition_broadcast(64)), "rb%d%d" % (par, half))
                    for half in range(2):
                        tt("dve", attnT[half * 64:(half + 1) * 64, jq, :], PS[ob + half][0:64, 0:TT], bcs[par][half][0:64, :], ALU.mult)
            MT = QT
            for m in range(8):
                wa = WS.next("w_a")
                for kc in range(8):
                    mm(PS[0][:, 0:TT], wa[:, kc, :], attnT[:, kc, :], start=(kc == 0), stop=(kc == 7))
                wsd = WS.next("w_s")
                for kc in range(16):
                    mm(PS[1][:, 0:TT], wsd[:, kc, :], ssdT[:, kc, :], start=(kc == 0), stop=(kc == 15))
                wg1 = WS.next("w_g")
                for kc in range(8):
                    mm(PS[2][:, 0:TT], wg1[:, kc, :], hT1[:, kc, :], start=(kc == 0), stop=(kc == 7))
                wg2 = WS.next("w_g")
                for kc in range(8):
                    mm(PS[3][:, 0:TT], wg2[:, kc, :], hT1[:, kc, :], start=(kc == 0), stop=(kc == 7))
                act(ga, PS[2][:, 0:TT], AF.Sigmoid, bias=cvs("gb", m))
                act(gs_, PS[3][:, 0:TT], AF.Sigmoid, bias=cvs("gb", 8 + m))
                tt("dve", m1, PS[0][:, 0:TT], ga, ALU.mult)
                tt("dve", m2, PS[1][:, 0:TT], gs_, ALU.mult)
                tt("pool", MT[:, m, :], m1, m2, ALU.add)
            if j + 1 < NTT:
                loadE_early(j + 1)
            for g in range(2):
                wo = WS.next("w_o")
                for mi in range(4):
                    m = g * 4 + mi
                    pb = m % 2
                    for kc in range(8):
                        mm(PS[pb][:, 0:TT], wo[:, kc, mi * 128:(mi + 1) * 128], MT[:, kc, :], start=(kc == 0), stop=(kc == 7))
                    tt("dve", xT[:, m, :], PS[pb][:, 0:TT], xT[:, m, :], ALU.add)
            dma(T(x1T_d[:, :, j * TT:(j + 1) * TT], "x1T_d%d" % j), xT, "eo0")
            rms_cols(sqb, lambda m: xT[:, m, :], 8, 7, rinv, TT, sdt)
            H2 = attnT
            for m in range(8):
                stt("dve", H2[:, m, :], xT[:, m, :], cvs("n2", m), rinv, ALU.mult, ALU.mult)
            dma(T(h2T_d[:, :, 1 + j * TT:1 + (j + 1) * TT], "h2T_d%d" % j), H2, "eo1")
        P.barrier(dum)

        A = phase_arena()
        WS = WStream(A)
        h2w = T(r3(A.bf16(8 * (TT + 2)), 8), "F_h2w")
        x1 = T(r3(A.f32(8 * TT), 8), "F_x1")
        pT = T(r3(A.bf16(2 * TT), 2), "F_pT")
        _g = A.f32(11 * TT)
        gT = T(r3(_g.bitcast(BF16), 22), "F_gT")
        yT = T(r3(_g[:, 0:8 * TT], 8), "F_gT")
        accg = T(A.f32(TT), "F_accg")
        accv = T(A.f32(TT), "F_accv")
        gl = T(A.f32(TT), "F_gl")
        x2b = T(r3(A.bf16(8 * TT), 8), "F_x2b")
        sg = T(A.f32(TT), "F_sg")
        sqb = [T(A.bf16(TT), "F_sqb%d" % i) for i in range(2)]
        rinv = T(A.f32(TT), "F_rinv")
        sdt = T(A.f32(TT), "F_sdt")
        yo = [T(A.f32(1024), "F_yo%d" % i) for i in range(2)]
        for j in range(NTT):
            for m in range(22):
                WS.plan([("w_up", 8, m * 128, 128), ("w_up", 8, 2816 + m * 128, 128)])
            WS.plan([("w_dn", 22, m * 128, 128) for m in range(8)])
            for m in range(8):
                WS.plan([("w_pg", 8, m * 128, 128), ("w_pl", 2, m * 128, 128)])
        load_hTw(h2w, h2T_d, 0, "fh")
        dma(pT, T(pT_d[:, :, 0:TT], "pT_d0"), "fp")
        for j in range(NTT):
            dma(x1, T(x1T_d[:, :, j * TT:(j + 1) * TT], "x1T_d%d" % j), "fx")
            for m in range(22):
                for half, accx in ((0, accg), (1, accv)):
                    wt = WS.next("w_up")
                    mc = m + 22 * half
                    pb = half
                    for kc in range(8):
                        mm(PS[pb][:, 0:TT], wt[:, kc, :], h2w[:, kc, 1:TT + 1], start=(kc == 0), stop=(kc == 7))
                    for kc in range(8):
                        mm(PS[2 + pb][:, 0:2], wt[:, kc, :], h2w[:, kc, 0:TT + 2:TT + 1], start=(kc == 0), stop=(kc == 7))
                    conv3(accx, PS[pb][:, 0:TT], PS[2 + pb][:, 0:2], cvs("fw0", mc), cvs("fw1", mc), cvs("fw2", mc), cvs("fb", mc))
                act(gl, accg, AF.Gelu_apprx_tanh)
                tt("pool", gT[:, m, :], gl, accv, ALU.mult)
            if j + 1 < NTT:
                load_hTw(h2w, h2T_d, j + 1, "fh")
            for m in range(8):
                wt = WS.next("w_dn")
                for kc in range(22):
                    mm(PS[4 + m % 2][:, 0:TT], wt[:, kc, :], gT[:, kc, :], start=(kc == 0), stop=(kc == 21))
                tt("dve", x1[:, m, :], PS[4 + m % 2][:, 0:TT], x1[:, m, :], ALU.add)
                cp("pool", x2b[:, m, :], x1[:, m, :])
            for m in range(8):
                wt = WS.next("w_pg")
                for kc in range(8):
                    mm(PS[6][:, 0:TT], wt[:, kc, :], x2b[:, kc, :], start=(kc == 0), stop=(kc == 7))
                wt = WS.next("w_pl")
                for kc in range(2):
                    mm(PS[7][:, 0:TT], wt[:, kc, :], pT[:, kc, :], start=(kc == 0), stop=(kc == 1))
                act(sg, PS[6][:, 0:TT], AF.Sigmoid, bias=cvs("bpg", m))
                tt("dve", sg, PS[7][:, 0:TT], sg, ALU.mult)
                tt("pool", x1[:, m, :], x1[:, m, :], sg, ALU.add)
            if j + 1 < NTT:
                dma(pT, T(pT_d[:, :, (j + 1) * TT:(j + 2) * TT], "pT_d%d" % (j + 1)), "fp")
            rms_cols(sqb, lambda m: x1[:, m, :], 8, 0, rinv, TT, sdt)
            for m in range(8):
                stt("dve", yT[:, m, :], x1[:, m, :], cvs("nf", m), rinv, ALU.mult, ALU.mult)
            for c in range(CPT):
                cg = j * CPT + c
                q = c % 2
                b0 = 1 + 2 * q
                for m in range(8):
                    tr(PS[b0 + m // 4][:, (m % 4) * 128:(m % 4 + 1) * 128], yT[:, m, c * 128:(c + 1) * 128], ident_f)
                cp("act", yo[q][:, 0:512], PS[b0])
                cp("dve", yo[q][:, 512:1024], PS[b0 + 1])
                dma(T(y_d[cg * 128:(cg + 1) * 128, :], "y_d%d" % cg), yo[q], "fo%d" % q)
        P.emit()
    return nc, P


def _consts():
    c = np.zeros((128, NCONST, 128), np.float32)
    i = np.arange(128)
    c[:, 0, :] = np.eye(128)
    pm = np.zeros((128, 128), np.float32)
    pm[i, i ^ 1] = 1.0
    c[:, 1, :] = pm
    c[:, 2, :] = (i[:, None] // 64 == i[None, :] // 64)
    c[:, 3, :] = 1.0
    c[:, 4, :] = i[:, None] <= i[None, :]
    c[:, 5, :] = i[:, None] > i[None, :]
    c[:, 6, :] = i[:, None] >= i[None, :]
    c[:, 7, :] = i[:, None] < i[None, :]
    c[64, 8, :] = 1.0
    return c


def _rope_tables(S):
    rows = S // 64
    row_idx = np.repeat(np.arange(rows, dtype=np.float32), 64)
    col_idx = np.tile(np.arange(64, dtype=np.float32), rows)
    inv_freq = (np.float32(10000.0) ** (-np.arange(0, 32, 2, dtype=np.float32) / np.float32(32))).astype(np.float32)
    ang = np.concatenate([row_idx[:, None] * inv_freq, col_idx[:, None] * inv_freq], axis=-1).astype(np.float32)
    cos, sin = np.cos(ang), np.sin(ang)
    pidx = (np.arange(128) % 64) // 2
    sign = np.where(np.arange(128) % 2 == 0, -1.0, 1.0).astype(np.float32)
    cosT = np.ascontiguousarray(cos[:, pidx].T).astype(np.float32)
    sinT = np.ascontiguousarray((sin[:, pidx] * sign[None, :]).T).astype(np.float32)
    return cosT, sinT


def _pack_weights(w):
    win = w["w_in"][0]
    o = {}
    q = win[:, 0:1024]
    qperm = []
    for pp in range(2):
        for r in range(4):
            for half in range(2):
                hq = (2 * pp + half) * 4 + r
                qperm.extend(range(hq * 64, hq * 64 + 64))
    qperm = np.array(qperm)
    o["w_q"] = q[:, qperm]
    o["w_k"] = win[:, 1024:1280]
    o["w_v"] = win[:, 1280:1536]
    o["w_z"] = win[:, 1536:3584]
    o["w_xbc"] = win[:, 3584:6656]
    o["w_dt"] = win[:, 6656:6720]
    o["w_g"] = win[:, 6720:8768]
    o["w_a"] = w["w_attn_branch"][0][qperm, :]
    o["w_s"] = w["w_ssd_branch"][0]
    o["w_o"] = w["w_out"][0]
    o["w_up"] = w["w_up"][0]
    o["w_dn"] = w["w_down"][0]
    o["w_pl"] = w["w_ple"][0]
    o["w_pg"] = w["w_ple_gate"][0]
    for n, k, m, mw in WSPEC:
        if mw is not None:
            o[n] = o[n].reshape(k // 128, 128, m // mw, mw).transpose(2, 1, 0, 3).reshape((m // mw) * 128, (k // 128) * mw)
    o = {k: np.ascontiguousarray(v, dtype=np.float32) for k, v in o.items()}
    cv = np.zeros((128, NCV), np.float32)

    def put(name, vec):
        off, wd = CV[name]
        v = np.asarray(vec, np.float32).reshape(-1)
        if v.size == 64 * wd and wd == 1:
            if name in ("qw", "kw"):
                cv[:, off] = np.concatenate([v, v])
            else:
                cv[0:64, off] = v
        else:
            cv[:, off:off + wd] = v.reshape(wd, 128).T

    put("n1", w["norm1_w"][0]); put("n2", w["norm2_w"][0]); put("nf", w["final_norm_w"])
    put("gb", w["gate_b"][0]); put("bpg", w["b_ple_gate"][0])
    put("qw", w["q_norm_w"][0]); put("kw", w["k_norm_w"][0])
    for t in range(3):
        put("cw%d" % t, w["ssm_conv_w"][0][t]); put("fw%d" % t, w["ffn_conv_w"][0][t])
    put("cb", w["ssm_conv_b"][0]); put("fb", w["ffn_conv_b"][0])
    put("dtb", w["dt_bias"][0].reshape(-1)); put("alog", w["a_log"][0].reshape(-1))
    o["cvec"] = cv
    o["rows"] = np.concatenate([w["norm1_w"][0], w["ssd_norm_w"][0], w["d_skip"][0]]).astype(np.float32)[None, :]
    o["consts"] = _consts()
    return o


_CACHE = {}


def run_seqs(xs, ps, w, S, debug=False, n_cores=None):
    if (S, debug) not in _CACHE:
        _CACHE[(S, debug)] = build(S, debug)
    nc, _ = _CACHE[(S, debug)]
    base = _pack_weights(w)
    base["cosT"], base["sinT"] = _rope_tables(S)
    in_maps = []
    for x, p in zip(xs, ps):
        m = dict(base)
        m["x"] = np.ascontiguousarray(x, dtype=np.float32)
        m["p"] = np.ascontiguousarray(p, dtype=np.float32)
        in_maps.append(m)
    res = run_bass_kernel_spmd(nc, in_maps, core_ids=list(range(len(in_maps))))
    return res.results


def kernel(**inputs):
    inp = {k: np.asarray(v) for k, v in inputs.items()}
    S = inp["x_prompt"].shape[1]
    xs = [inp["x_prompt"][b] for b in range(4)] + [inp["x_sample"][b] for b in range(2)]
    ps = [inp["p_prompt"][0, b] for b in range(4)] + [inp["p_sample"][0, b] for b in range(2)]
    xs += [xs[0], xs[1]]
    ps += [ps[0], ps[1]]
    res = run_seqs(xs, ps, inp, S)
    y_prompt = np.stack([res[b]["y"] for b in range(4)]).astype(np.float32)
    y_sample = np.stack([res[4 + b]["y"] for b in range(2)]).astype(np.float32)
    return (y_prompt, y_sample)
```
